# Optimizing a Trainium2 kernel written in Bass

```python
import jax
import jax.numpy as jnp
from jax import lax
import numpy as np

D_MODEL = 2048
BATCH = 2
SEQ = 8192
DEPTH = 2

GRID_W = 64
CTX_LEN = 256
N_MOD = 6
D_FF = 4 * D_MODEL
NORM_EPS = 1e-6

A_WIDTH = D_MODEL // 2
A_HEAD_DIM = 64
A_HEADS = A_WIDTH // A_HEAD_DIM
A_LORA_W = 96
A_LORA_A = 96
A_LORA_G = 256
A_LN_EPS = 64e-5
A_COLS = 3 * A_WIDTH + A_LORA_G + 2 * A_LORA_W + 2 * A_LORA_A

B_HEADS = 4
B_WIDTH_V = D_MODEL // 2
B_DV = B_WIDTH_V // B_HEADS
B_DK = B_DV // 2
B_WIDTH_K = B_HEADS * B_DK
B_LORA = 16
B_GATE_TAU = 16.0
B_CHUNK = 64
B_COLS = 2 * B_WIDTH_K + 2 * B_WIDTH_V + 2 * B_LORA
AB_COLS = A_COLS + B_COLS

C_WIDTH = D_MODEL
C_BLOCKS = 8
C_BLOCK = C_WIDTH // C_BLOCKS
C_CONV = 4
C_CONST = 8.0

kernel_name = "hybrid_rwkv7_gla_rglru_dit_block"


def rms_norm(x, g):
    xf = x.astype(jnp.float32)
    y = xf * lax.rsqrt(jnp.mean(xf * xf, axis=-1, keepdims=True) + NORM_EPS)
    return (y * g.astype(jnp.float32)).astype(x.dtype)


def seg_flip(t, n_ctx):
    return jnp.concatenate([jnp.flip(t[:, :n_ctx], 1), jnp.flip(t[:, n_ctx:], 1)], axis=1)


def shift_mix(p, mu):
    prev = jnp.pad(p[:, :-1], ((0, 0), (1, 0), (0, 0)))
    nxt = jnp.pad(p[:, 1:], ((0, 0), (0, 1), (0, 0)))
    return p + mu[0] * (prev - p) + mu[1] * (nxt - p)


def rwkv7_scan(r, w, k, v, a, b):
    bsz, _, h, n = r.shape

    def step(state, inp):
        r_t, w_t, k_t, v_t, a_t, b_t = inp
        sa = jnp.einsum('bhvk,bhk->bhv', state, a_t)
        state = (state * w_t[:, :, None, :] + sa[..., None] * b_t[:, :, None, :]
                 + v_t[..., None] * k_t[:, :, None, :])
        return state, jnp.einsum('bhvk,bhk->bhv', state, r_t)

    s0 = jnp.zeros((bsz, h, n, n), jnp.float32)
    xs = tuple(jnp.moveaxis(z, 1, 0) for z in (r, w, k, v, a, b))
    _, y = lax.scan(step, s0, xs)
    return jnp.moveaxis(y, 0, 1)


def rwkv7_mixer(pc, pl, mu, w0, w2, a0, a2, g2, k_k, k_a, r_k, ln_w, ln_b):
    n_ctx = pc.shape[1]
    p = jnp.concatenate([shift_mix(pc, mu), shift_mix(pl, mu)], axis=1).astype(jnp.float32)
    bsz, t = p.shape[:2]
    hs = (bsz, t, A_HEADS, A_HEAD_DIM)
    idx = [A_WIDTH, 2 * A_WIDTH, 3 * A_WIDTH, 3 * A_WIDTH + A_LORA_G,
           3 * A_WIDTH + A_LORA_G + 2 * A_LORA_W]
    r, k, v, gd, wd, ad = jnp.split(p, idx, axis=-1)
    wd = wd.reshape(bsz, t, 2, A_LORA_W)
    ad = ad.reshape(bsz, t, 2, A_LORA_A)
    g = jax.nn.sigmoid(gd) @ g2
    kk = (k * k_k).reshape(hs)
    kk = kk / jnp.maximum(jnp.sqrt(jnp.sum(kk * kk, -1, keepdims=True)), 1e-12)
    rh = r.reshape(hs)
    vh = v.reshape(hs)

    def direction(d):
        w_log = -jax.nn.softplus(-(w0[d] + jnp.tanh(wd[:, :, d]) @ w2[d])) - 0.5
        decay = jnp.exp(-jnp.exp(w_log)).reshape(hs)
        a = jax.nn.sigmoid(a0[d] + ad[:, :, d] @ a2[d])
        kd = (k * (1.0 + (a - 1.0) * k_a)).reshape(hs)
        ins = (rh, decay, kd, vh, -kk, kk * a.reshape(hs))
        if d == 1:
            yd = seg_flip(rwkv7_scan(*(seg_flip(z, n_ctx) for z in ins)), n_ctx)
        else:
            yd = rwkv7_scan(*ins)
        bonus = jnp.sum(rh * kd * r_k, -1, keepdims=True) * vh
        return yd, bonus

    y_f, bonus_f = direction(0)
    y_b, bonus_b = direction(1)
    y = y_f + y_b
    mean = jnp.mean(y, -1, keepdims=True)
    var = jnp.mean(jnp.square(y - mean), -1, keepdims=True)
    y = ((y - mean) * lax.rsqrt(var + A_LN_EPS)).reshape(bsz, t, A_WIDTH) * ln_w + ln_b
    y = (y + (bonus_f + bonus_b).reshape(bsz, t, A_WIDTH)) * g
    return y[:, :n_ctx], y[:, n_ctx:]


def gla_chunked(q, k, v, log_a):
    bsz, t, h, dk = q.shape
    dv = v.shape[-1]
    n = t // B_CHUNK
    q, k, log_a = (z.reshape(bsz, n, B_CHUNK, h, dk) for z in (q, k, log_a))
    v = v.reshape(bsz, n, B_CHUNK, h, dv)
    cum = jnp.cumsum(log_a, axis=2)
    q_in = q * jnp.exp(cum)
    k_in = k * jnp.exp(-cum)
    lower = jnp.tril(jnp.ones((B_CHUNK, B_CHUNK), dtype=bool))
    scores = jnp.where(lower, jnp.einsum('bnihk,bnjhk->bnhij', q_in, k_in), 0.0)
    intra = jnp.einsum('bnhij,bnjhv->bnihv', scores, v)
    cum_last = cum[:, :, -1]
    k_end = k * jnp.exp(cum_last[:, :, None] - cum)
    chunk_kv = jnp.einsum('bnjhk,bnjhv->bnhkv', k_end, v)
    chunk_decay = jnp.exp(cum_last)

    def step(state, inp):
        kv, dec = inp
        return state * dec[..., None] + kv, state

    s0 = jnp.zeros((bsz, h, dk, dv), jnp.float32)
    _, s_in = lax.scan(step, s0, (jnp.moveaxis(chunk_kv, 1, 0), jnp.moveaxis(chunk_decay, 1, 0)))
    inter = jnp.einsum('bnihk,nbhkv->bnihv', q_in, s_in)
    return (intra + inter).reshape(bsz, t, h, dv)


def gla_mixer(pc, pl, gw2, gb, norm_g):
    n_ctx = pc.shape[1]
    p = jnp.concatenate([pc, pl], axis=1).astype(jnp.float32)
    bsz, t = p.shape[:2]
    idx = [B_WIDTH_K, 2 * B_WIDTH_K, 2 * B_WIDTH_K + B_WIDTH_V, 2 * B_WIDTH_K + 2 * B_WIDTH_V]
    q, k, v, g, gd = jnp.split(p, idx, axis=-1)
    q = q.reshape(bsz, t, B_HEADS, B_DK) * (B_DK ** -0.5)
    k = k.reshape(bsz, t, B_HEADS, B_DK)
    v = v.reshape(bsz, t, B_HEADS, B_DV)
    gd = gd.reshape(bsz, t, 2, B_LORA)

    def log_gate(d):
        z = gd[:, :, d] @ gw2[d] + gb[d]
        return (jax.nn.log_sigmoid(z) / B_GATE_TAU).reshape(bsz, t, B_HEADS, B_DK)

    o_f = gla_chunked(q, k, v, log_gate(0))
    o_b = seg_flip(gla_chunked(*(seg_flip(z, n_ctx) for z in (q, k, v, log_gate(1)))), n_ctx)
    o = o_f + o_b
    o = o * lax.rsqrt(jnp.mean(o * o, -1, keepdims=True) + NORM_EPS) * norm_g
    o = o.reshape(bsz, t, B_WIDTH_V) * jax.nn.silu(g)
    return o[:, :n_ctx], o[:, n_ctx:]


def dwconv_centred(x, w, b):
    t = x.shape[1]
    left = C_CONV // 2
    xp = jnp.pad(x, ((0, 0), (left, C_CONV - 1 - left), (0, 0)))
    y = b + xp[:, 0:t] * w[0]
    for j in range(1, C_CONV):
        y = y + xp[:, j:j + t] * w[j]
    return y


def linear_scan(a, u):
    def combine(left, right):
        a_l, u_l = left
        a_r, u_r = right
        return a_l * a_r, a_r * u_l + u_r
    return lax.associative_scan(combine, (a, u), axis=1)[1]


def to_col_major(t, rows):
    b, _, ch = t.shape
    return t.reshape(b, rows, GRID_W, ch).transpose(0, 2, 1, 3).reshape(b, rows * GRID_W, ch)


def from_col_major(t, rows):
    b, _, ch = t.shape
    return t.reshape(b, GRID_W, rows, ch).transpose(0, 2, 1, 3).reshape(b, rows * GRID_W, ch)


def rglru_mixer(hc, hl, w_in, conv_w, conv_b, wa, ba, wx, bx, lam, rows):
    n_ctx = hc.shape[1]
    gate_c, xc = jnp.split(hc @ w_in, 2, axis=-1)
    gate_l, xl = jnp.split(hl @ w_in, 2, axis=-1)
    xs = jnp.concatenate([dwconv_centred(xc, conv_w, conv_b),
                          dwconv_centred(to_col_major(xl, rows), conv_w, conv_b)],
                         axis=1).astype(jnp.float32)
    bsz, t, _ = xs.shape
    xb = xs.reshape(bsz, t, C_BLOCKS, C_BLOCK)

    def direction(d):
        r = jax.nn.sigmoid(jnp.einsum('btnc,ncd->btnd', xb, wa[d]).reshape(bsz, t, C_WIDTH) + ba[d])
        i = jax.nn.sigmoid(jnp.einsum('btnc,ncd->btnd', xb, wx[d]).reshape(bsz, t, C_WIDTH) + bx[d])
        log_a = -C_CONST * r * jax.nn.softplus(-lam[d])
        a = jnp.exp(log_a)
        u = jnp.sqrt(-jnp.expm1(2.0 * log_a)) * (i * xs)
        if d == 1:
            return seg_flip(linear_scan(seg_flip(a, n_ctx), seg_flip(u, n_ctx)), n_ctx)
        return linear_scan(a, u)

    h = direction(0) + direction(1)
    yc = h[:, :n_ctx] * jax.nn.gelu(gate_c.astype(jnp.float32))
    yl = from_col_major(h[:, n_ctx:], rows) * jax.nn.gelu(gate_l.astype(jnp.float32))
    return yc, yl


def sq_relu_mlp(h, w1, w2):
    return jnp.square(jax.nn.relu(h @ w1)) @ w2


def setup_inputs(seed: int = 0) -> dict:
    key = jax.random.key(seed)
    ks = iter(jax.random.split(key, 48))
    D = D_MODEL
    ne = (DEPTH + 1) // 2
    no = DEPTH // 2

    def nrm(shape, scale):
        return jax.random.normal(next(ks), shape, jnp.float32) * scale

    def uni(shape, lo, hi):
        return jax.random.uniform(next(ks), shape, jnp.float32, minval=lo, maxval=hi)

    inp = {}
    inp["x"] = nrm((BATCH, SEQ, D), 1.0)
    inp["c"] = nrm((BATCH, D), 1.0)
    inp["ctx"] = nrm((BATCH, CTX_LEN, D), 1.0)
    inp["c_ctx"] = nrm((D,), 1.0)
    inp["mod_w"] = nrm((DEPTH, D, N_MOD * D), 0.5 * D ** -0.5)
    inp["mod_b"] = nrm((DEPTH, N_MOD * D), 0.02)
    inp["norm1"] = 1.0 + nrm((DEPTH, D), 0.02)
    inp["norm2"] = 1.0 + nrm((DEPTH, D), 0.02)
    inp["mlp_w1"] = nrm((DEPTH, D, D_FF), D ** -0.5)
    inp["mlp_w2"] = nrm((DEPTH, D_FF, D), D_FF ** -0.5)
    inp["ab_w_in"] = nrm((ne, D, AB_COLS), D ** -0.5)
    inp["ab_w_out"] = nrm((ne, A_WIDTH + B_WIDTH_V, D), (A_WIDTH + B_WIDTH_V) ** -0.5)
    inp["rw_mu"] = uni((ne, 2, A_COLS), 0.0, 0.5)
    inp["rw_w0"] = uni((ne, 2, A_WIDTH), -7.0, -2.0)
    inp["rw_w2"] = nrm((ne, 2, A_LORA_W, A_WIDTH), 0.5 * A_LORA_W ** -0.5)
    inp["rw_a0"] = nrm((ne, 2, A_WIDTH), 0.1)
    inp["rw_a2"] = nrm((ne, 2, A_LORA_A, A_WIDTH), 0.5 * A_LORA_A ** -0.5)
    inp["rw_g2"] = nrm((ne, A_LORA_G, A_WIDTH), A_LORA_G ** -0.5)
    inp["rw_kk"] = 0.85 + nrm((ne, A_WIDTH), 0.05)
    inp["rw_ka"] = 1.0 + nrm((ne, A_WIDTH), 0.05)
    inp["rw_rk"] = nrm((ne, A_HEADS, A_HEAD_DIM), 0.1)
    inp["rw_ln_w"] = 1.0 + nrm((ne, A_WIDTH), 0.02)
    inp["rw_ln_b"] = nrm((ne, A_WIDTH), 0.02)
    inp["gla_gw2"] = nrm((ne, 2, B_LORA, B_WIDTH_K), 0.5 * B_LORA ** -0.5)
    inp["gla_gb"] = uni((ne, 2, B_WIDTH_K), 1.0, 4.0)
    inp["gla_norm"] = 1.0 + nrm((ne, B_DV), 0.02)
    inp["lru_w_in"] = nrm((no, D, 2 * C_WIDTH), D ** -0.5)
    inp["lru_w_out"] = nrm((no, C_WIDTH, D), C_WIDTH ** -0.5)
    inp["lru_conv_w"] = nrm((no, C_CONV, C_WIDTH), C_CONV ** -0.5)
    inp["lru_conv_b"] = nrm((no, C_WIDTH), 0.02)
    inp["lru_wa"] = nrm((no, 2, C_BLOCKS, C_BLOCK, C_BLOCK), C_BLOCK ** -0.5)
    inp["lru_ba"] = nrm((no, 2, C_WIDTH), 0.02)
    inp["lru_wx"] = nrm((no, 2, C_BLOCKS, C_BLOCK, C_BLOCK), C_BLOCK ** -0.5)
    inp["lru_bx"] = nrm((no, 2, C_WIDTH), 0.02)
    a8 = uni((no, 2, C_WIDTH), 0.9, 0.999)
    s = a8 ** (1.0 / C_CONST)
    inp["lru_lam"] = jnp.log(s) - jnp.log1p(-s)
    inp["final_norm"] = 1.0 + nrm((D,), 0.02)
    return inp


def reference(x, c, ctx, c_ctx, mod_w, mod_b, norm1, norm2, mlp_w1, mlp_w2,
              ab_w_in, ab_w_out, rw_mu, rw_w0, rw_w2, rw_a0, rw_a2, rw_g2, rw_kk, rw_ka, rw_rk,
              rw_ln_w, rw_ln_b, gla_gw2, gla_gb, gla_norm,
              lru_w_in, lru_w_out, lru_conv_w, lru_conv_b, lru_wa, lru_ba, lru_wx, lru_bx, lru_lam,
              final_norm):
    dt = x.dtype
    rows = x.shape[1] // GRID_W
    silu_c = jax.nn.silu(c)[:, None, :]
    silu_cc = jax.nn.silu(c_ctx)[None, None, :]
    xl, xc = x, ctx
    for i in range(DEPTH):
        last = i == DEPTH - 1
        j = i // 2
        m_l = jnp.split(silu_c @ mod_w[i] + mod_b[i], N_MOD, axis=-1)
        m_c = jnp.split(silu_cc @ mod_w[i] + mod_b[i], N_MOD, axis=-1)
        hl = rms_norm(xl, norm1[i]) * (1.0 + m_l[1]) + m_l[0]
        hc = rms_norm(xc, norm1[i]) * (1.0 + m_c[1]) + m_c[0]
        if i % 2 == 0:
            pc = hc @ ab_w_in[j]
            pl = hl @ ab_w_in[j]
            ac, al = rwkv7_mixer(pc[..., :A_COLS], pl[..., :A_COLS], rw_mu[j], rw_w0[j], rw_w2[j],
                                 rw_a0[j], rw_a2[j], rw_g2[j], rw_kk[j], rw_ka[j], rw_rk[j],
                                 rw_ln_w[j], rw_ln_b[j])
            bc, bl = gla_mixer(pc[..., A_COLS:], pl[..., A_COLS:], gla_gw2[j], gla_gb[j], gla_norm[j])
            out_l = jnp.concatenate([al, bl], axis=-1).astype(dt) @ ab_w_out[j]
            if not last:
                out_c = jnp.concatenate([ac, bc], axis=-1).astype(dt) @ ab_w_out[j]
        else:
            yc, yl = rglru_mixer(hc, hl, lru_w_in[j], lru_conv_w[j], lru_conv_b[j], lru_wa[j],
                                 lru_ba[j], lru_wx[j], lru_bx[j], lru_lam[j], rows)
            out_l = yl.astype(dt) @ lru_w_out[j]
            if not last:
                out_c = yc.astype(dt) @ lru_w_out[j]
        xl = xl + m_l[2] * out_l
        hl = rms_norm(xl, norm2[i]) * (1.0 + m_l[4]) + m_l[3]
        xl = xl + m_l[5] * sq_relu_mlp(hl, mlp_w1[i], mlp_w2[i])
        if not last:
            xc = xc + m_c[2] * out_c
            hc = rms_norm(xc, norm2[i]) * (1.0 + m_c[4]) + m_c[3]
            xc = xc + m_c[5] * sq_relu_mlp(hc, mlp_w1[i], mlp_w2[i])
    return rms_norm(xl, final_norm)
```

```python
import numpy as np
import ml_dtypes
from contextlib import ExitStack
import concourse.bass as bass
import concourse.mybir as mybir
from concourse.bass_utils import run_bass_kernel_spmd

F32 = mybir.dt.float32
BF16 = mybir.dt.bfloat16
AF = mybir.ActivationFunctionType
ALU = mybir.AluOpType
AX = mybir.AxisListType
NPBF = ml_dtypes.bfloat16

D = 2048
DFF = 8192
NCORES = 8
EPS = 1e-6


class Buf:
    __slots__ = ("name", "lw", "rd", "sem", "cnt")

    def __init__(self, name):
        self.name = name
        self.lw = None
        self.rd = []
        self.sem = None
        self.cnt = 0


class Tile:
    def __init__(self, t, bufs):
        self.t = t
        self.b = bufs

    def __getitem__(self, k):
        return self.t[k]


class Sched:
    ENG = ["pe", "act", "dve", "pool", "sp"]

    def __init__(self, nc, stack):
        self.nc = nc
        self.stack = stack
        self.esem = {e: stack.enter_context(nc.semaphore("s_" + e)) for e in self.ENG}
        self.ecnt = {e: 0 for e in self.ENG}
        self.ops = {e: [] for e in self.ENG}
        self.waited = {e: {} for e in self.ENG}
        self.nsem = 0
        self.same_engine_sync = True

    def tile(self, name, shape, dtype, nbuf=1, psum=False):
        if psum:
            t = self.stack.enter_context(self.nc.psum_tensor("pp_" + name, shape, dtype))
        else:
            t = self.stack.enter_context(self.nc.sbuf_tensor("sb_" + name, shape, dtype))
        return Tile(t, [Buf(f"{name}.{i}") for i in range(nbuf)])

    def tile_cache(self, name, shape, dtype):
        if not hasattr(self, "_tc"):
            self._tc = {}
        if name not in self._tc:
            self._tc[name] = self.tile(name, shape, dtype)
        return self._tc[name]

    def _conds(self, reads, writes):
        conds = []
        for b in reads:
            if b.lw is not None:
                conds.append(b.lw)
        for b in writes:
            if b.lw is not None:
                conds.append(b.lw)
            conds.extend(b.rd)
        return conds

    def _filter(self, eng, conds):
        waits = []
        w = self.waited[eng]
        best = {}
        for (sem, val, src, key) in conds:
            if src == eng and (eng == "pe" or not self.same_engine_sync):
                continue
            if key not in best or best[key][1] < val:
                best[key] = (sem, val)
        for key, (sem, val) in best.items():
            if w.get(key, 0) >= val:
                continue
            w[key] = val
            waits.append((sem, val))
        return waits

    def _update(self, reads, writes, done):
        for b in writes:
            b.lw = done
            b.rd = []
        for b in reads:
            b.rd.append(done)

    def op(self, eng, fn, reads=(), writes=()):
        waits = self._filter(eng, self._conds(reads, writes))
        self.ecnt[eng] += 1
        done = (self.esem[eng], self.ecnt[eng], eng, "e_" + eng)
        self.ops[eng].append((waits, fn, (self.esem[eng], 1)))
        self._update(reads, writes, done)

    def dma(self, q, fn, reads, writes, anchor):
        waits = self._filter(q, self._conds(reads, writes))
        cls = "sw" if q == "pool" else "hw"
        if anchor.sem is None:
            anchor.sem = {}
            anchor.cnt = {}
        if cls not in anchor.sem:
            self.nsem += 1
            anchor.sem[cls] = self.stack.enter_context(self.nc.semaphore(f"d{self.nsem}"))
            anchor.cnt[cls] = 0
        anchor.cnt[cls] += 16
        done = (anchor.sem[cls], anchor.cnt[cls], "dma", "d_" + cls + anchor.name)
        self.ops[q].append((waits, fn, (anchor.sem[cls], 16)))
        self._update(reads, writes, done)

    def wait_all(self, eng, bufs):
        conds = []
        for b in bufs:
            if b.lw is not None:
                conds.append(b.lw)
            conds.extend(b.rd)
        waits = self._filter(eng, conds)
        self.ops[eng].append((waits, None, None))

    def emit(self):
        nc = self.nc
        with nc.Block() as block:
            def mk(ops):
                def body(e):
                    for waits, fn, inc in ops:
                        for (sem, val) in waits:
                            e.wait_ge(sem, val)
                        if fn is not None:
                            fn(e).then_inc(inc[0], inc[1])
                return body
            block.sync(mk(self.ops["sp"]))
            block.scalar(mk(self.ops["act"]))
            block.vector(mk(self.ops["dve"]))
            block.gpsimd(mk(self.ops["pool"]))
            block.tensor(mk(self.ops["pe"]))


CW = 4096
MODC = 3072


def build_prep(FW):
    nc = bass.Bass("TRN2", target_bir_lowering=False)
    wf = nc.dram_tensor("wf", [128, FW], F32, kind="ExternalInput").ap()
    cT = nc.dram_tensor("cT", [128, 48], F32, kind="ExternalInput").ap()
    modw = nc.dram_tensor("modw", [D, MODC], F32, kind="ExternalInput").ap()
    modb = nc.dram_tensor("modb", [1, MODC], F32, kind="ExternalInput").ap()
    wb = nc.dram_tensor("wb", [128, FW], BF16, kind="ExternalOutput").ap()
    mo = nc.dram_tensor("m", [3, MODC], F32, kind="ExternalOutput").ap()
    with ExitStack() as st:
        S = Sched(nc, st)
        NB = 3
        tin = S.tile("tin", [128, NB, CW], F32, NB)
        tout = S.tile("tout", [128, NB, CW], BF16, NB)
        cs = S.tile("cs", [128, 48], F32)
        sc = S.tile("sc", [128, 48], F32)
        mw = S.tile("mw", [128, 2, 16, 512], F32, 2)
        mb = S.tile("mb", [3, MODC], F32)
        ms = S.tile("ms", [3, MODC], F32)
        ps = S.tile("ps", [128, 2, 512], F32, 2, psum=True)
        S.dma("pool", lambda e: e.dma_start(out=cs[:], in_=cT[:, :]), [], [cs.b[0]], cs.b[0])
        S.dma("pool", lambda e: e.dma_start(out=mb[:], in_=modb.partition_broadcast(3)), [], [mb.b[0]], mb.b[0])
        S.op("act", lambda e: e.activation(out=sc[:], in_=cs[:], func=AF.Silu), [cs.b[0]], [sc.b[0]])
        modw_v = modw.rearrange("(c p) n -> p c n", p=128)
        for g in range(MODC // 512):
            k = g % 2
            S.dma("pool", lambda e, g=g, k=k: e.dma_start(out=mw[:, k], in_=modw_v[:, :, g * 512:(g + 1) * 512]),
                  [], [mw.b[k]], mw.b[k])
            for c in range(16):
                S.op("pe", lambda e, c=c, k=k: e.matmul(out=ps[0:3, k, :], lhsT=sc[:, c * 3:(c + 1) * 3], rhs=mw[:, k, c, :],
                                                       start=(c == 0), stop=(c == 15)),
                     [sc.b[0], mw.b[k]], [ps.b[k]])
            S.op("dve", lambda e, g=g, k=k: e.tensor_tensor(out=ms[:, g * 512:(g + 1) * 512], in0=ps[0:3, k, :],
                                                           in1=mb[:, g * 512:(g + 1) * 512], op=ALU.add),
                 [ps.b[k], mb.b[0]], [ms.b[0]])
        S.dma("pool", lambda e: e.dma_start(out=mo[:, :], in_=ms[:]), [ms.b[0]], [], ms.b[0])
        nch = (FW + CW - 1) // CW
        engs = ["act", "dve", "pool"]
        for i in range(nch):
            k = i % NB
            c0 = i * CW
            w = min(CW, FW - c0)
            S.dma("sp", lambda e, k=k, c0=c0, w=w: e.dma_start(out=tin[:, k, 0:w], in_=wf[:, c0:c0 + w]),
                  [], [tin.b[k]], tin.b[k])
            eng = engs[i % 3]
            if eng == "act":
                S.op("act", lambda e, k=k, w=w: e.activation(out=tout[:, k, 0:w], in_=tin[:, k, 0:w], func=AF.Copy),
                     [tin.b[k]], [tout.b[k]])
            else:
                S.op(eng, lambda e, k=k, w=w: e.tensor_copy(out=tout[:, k, 0:w], in_=tin[:, k, 0:w]),
                     [tin.b[k]], [tout.b[k]])
            S.dma("sp", lambda e, k=k, c0=c0, w=w: e.dma_start(out=wb[:, c0:c0 + w], in_=tout[:, k, 0:w]),
                  [tout.b[k]], [], tout.b[k])
        S.wait_all("sp", tout.b + ms.b)
        S.emit()
    return nc


TT = 256


def build_tok(mode, n_lat, n_ctx):
    nc = bass.Bass("TRN2", target_bir_lowering=False)
    full = mode in ("mid", "last")
    classes = [("lat", n_lat)] + ([("ctx", n_ctx)] if n_ctx else [])
    dr = {}
    for cn, n in classes:
        dr["x_" + cn] = nc.dram_tensor("x_" + cn, [n, D], F32, kind="ExternalInput").ap()
        dr["vec_" + cn] = nc.dram_tensor("vec_" + cn, [8, D], F32, kind="ExternalInput").ap()
        if full:
            dr["y_" + cn] = nc.dram_tensor("y_" + cn, [D, n], BF16, kind="ExternalInput").ap()
        if mode == "last":
            dr["out_" + cn] = nc.dram_tensor("out_" + cn, [n, D], F32, kind="ExternalOutput").ap()
        else:
            dr["h_" + cn] = nc.dram_tensor("h_" + cn, [D, n], BF16, kind="ExternalOutput").ap()
            if full:
                dr["xo_" + cn] = nc.dram_tensor("xo_" + cn, [n, D], F32, kind="ExternalOutput").ap()
    ident_d = nc.dram_tensor("ident", [128, 128], BF16, kind="ExternalInput").ap()
    if full:
        wout = nc.dram_tensor("wout", [D, D], BF16, kind="ExternalInput").ap()
        w1t = nc.dram_tensor("w1t", [64, 128, D], BF16, kind="ExternalInput").ap()
        w2 = nc.dram_tensor("w2", [DFF, D], BF16, kind="ExternalInput").ap()
    with ExitStack() as st:
        S = Sched(nc, st)
        ident = S.tile("ident", [128, 128], BF16)
        S.dma("pool", lambda e: e.dma_start(out=ident[:], in_=ident_d[:, :]), [], [ident.b[0]], ident.b[0])
        NXB = 2
        X = S.tile("X", [128, NXB, 2, D], F32, NXB)
        C = S.tile("C", [128, 6, D], F32, 6)
        NTMP = 3
        tmp = S.tile("tmp", [128, NTMP, D], F32, NTMP)
        vt = tmp
        htm = S.tile("htm", [128, 2, D], BF16, 2)
        stat = S.tile("stat", [128, 8, 4], F32, 8)
        HF = S.tile("HF", [128, 2, 16, TT], BF16, 2)
        PT = S.tile("PT", [128, 2, 1024], BF16, 2, psum=True)
        NPS = 6
        PS = [S.tile(f"PS{i}", [128, 512], F32, 1, psum=True) for i in range(NPS)]
        psi = [0]

        def nps():
            p = PS[psi[0] % NPS]
            psi[0] += 1
            return p
        if full:
            Y = S.tile("Y", [128, 2, 16, TT], BF16, 2)
            A = S.tile("A", [128, 64, TT], BF16)
            NWB = 5
            WR = S.tile("WR", [128, NWB, D], BF16, NWB)
            rl = S.tile("rl", [128, 3, TT], F32, 3)
            ev = S.tile("ev", [128, 3, 512], F32, 3)
        cnt = {"w": 0, "st": 0, "rl": 0, "ev": 0, "x": 0, "hf": 0, "tmp": 0, "jk": 0}

        def rot(name, n):
            k = cnt[name] % n
            cnt[name] += 1
            return k

        def load_row(vec, r):
            k = rot("tmp", NTMP)
            S.dma("pool", lambda e: e.dma_start(out=vt[:, k], in_=vec[r:r + 1, :].partition_broadcast(128)),
                  [], [vt.b[k]], vt.b[k])
            return k

        def setup_consts(vec):
            def cp(dst, r):
                S.dma("pool", lambda e: e.dma_start(out=C[:, dst], in_=vec[r:r + 1, :].partition_broadcast(128)),
                      [], [C.b[dst]], C.b[dst])

            def gmul(dst, rs, rn):
                k1 = load_row(vec, rs)
                k2 = load_row(vec, rn)
                S.op("dve", lambda e: e.scalar_tensor_tensor(out=C[:, dst], in0=vt[:, k1], scalar=1.0, in1=vt[:, k2],
                                                              op0=ALU.add, op1=ALU.mult),
                     [vt.b[k1], vt.b[k2]], [C.b[dst]])
            if full:
                cp(0, 0)
                gmul(1, 2, 4)
                cp(2, 1)
                cp(3, 3)
            gmul(4, 6, 7)
            cp(5, 5)

        def modnorm(src_ap, src_buf, gi, si, out_ap, out_buf):
            sk = rot("st", 8)
            tk = rot("tmp", NTMP)
            if out_ap is None:
                out_ap, out_buf = tmp[0:src_ap.shape[0], tk], tmp.b[tk]
            npr = src_ap.shape[0]
            jk = rot("jk", 2)
            S.op("act", lambda e: e.activation(out=htm[0:npr, jk], in_=src_ap, func=AF.Square, accum_out=stat[0:npr, sk, 0:1]),
                 [src_buf], [htm.b[jk], stat.b[sk]])
            S.op("dve", lambda e: e.tensor_scalar(out=stat[0:npr, sk, 1:2], in0=stat[0:npr, sk, 0:1], scalar1=1.0 / D, scalar2=EPS,
                                                 op0=ALU.mult, op1=ALU.add), [stat.b[sk]], [stat.b[sk]])
            S.op("act", lambda e: e.activation(out=stat[0:npr, sk, 2:3], in_=stat[0:npr, sk, 1:2], func=AF.Sqrt),
                 [stat.b[sk]], [stat.b[sk]])
            S.op("dve", lambda e: e.reciprocal(out=stat[0:npr, sk, 3:4], in_=stat[0:npr, sk, 2:3]), [stat.b[sk]], [stat.b[sk]])
            npq = src_ap.shape[0]
            S.op("dve", lambda e: e.scalar_tensor_tensor(out=tmp[0:npq, tk], in0=src_ap, scalar=stat[0:npq, sk, 3:4], in1=C[0:npq, gi],
                                                         op0=ALU.mult, op1=ALU.mult),
                 [src_buf, stat.b[sk], C.b[gi]], [tmp.b[tk]])
            npp = src_ap.shape[0]
            S.op("pool", lambda e: e.tensor_tensor(out=out_ap, in0=tmp[0:npp, tk], in1=C[0:npp, si], op=ALU.add),
                 [tmp.b[tk], C.b[si]], [out_buf])
            return tk

        def transpose_to(hk, np_, s, hfk):
            for q in range(4):
                h = q % 2
                for j in range(4):
                    c = q * 4 + j
                    S.op("pe", lambda e, c=c, j=j, h=h: e.transpose(out=PT[:, h, j * 128:j * 128 + np_],
                                                                     in_=htm[0:np_, hk, c * 128:(c + 1) * 128],
                                                                     identity=ident[0:np_, 0:np_]),
                         [htm.b[hk], ident.b[0]], [PT.b[h]])
                eng = "act" if q % 2 == 0 else "dve"
                src = PT[:, h, 0:512].rearrange("p (c t) -> p c t", t=128)[:, :, 0:np_]
                dst = HF[:, hfk, q * 4:(q + 1) * 4, s * 128:s * 128 + np_]
                if eng == "act":
                    S.op("act", lambda e, src=src, dst=dst: e.activation(out=dst, in_=src, func=AF.Copy),
                         [PT.b[h]], [HF.b[hfk]])
                else:
                    S.op("dve", lambda e, src=src, dst=dst: e.tensor_copy(out=dst, in_=src), [PT.b[h]], [HF.b[hfk]])

        def wload(src_ap, width):
            k = rot("w", NWB)
            S.dma("sp", lambda e: e.dma_start(out=WR[:, k, 0:width], in_=src_ap), [], [WR.b[k]], WR.b[k])
            return k

        def gemm_tm(lhs_fn, lhs_buf, KC, wsrc_fn, gate_idx, xk, subs):
            for half in range(2):
                banks = {}
                for (s, np_) in subs:
                    for dch in range(2):
                        banks[(s, dch)] = nps()
                for c in range(KC):
                    wk = wload(wsrc_fn(c, half), 1024)
                    for (s, np_) in subs:
                        for dch in range(2):
                            pb = banks[(s, dch)]
                            S.op("pe", lambda e, pb=pb, c=c, s=s, np_=np_, dch=dch, wk=wk: e.matmul(
                                out=pb[0:np_, :], lhsT=lhs_fn(c, s, np_), rhs=WR[:, wk, dch * 512:(dch + 1) * 512],
                                start=(c == 0), stop=(c == KC - 1)), [lhs_buf, WR.b[wk]], [pb.b[0]])
                for (s, np_) in subs:
                    for dch in range(2):
                        pb = banks[(s, dch)]
                        d0 = half * 1024 + dch * 512
                        ek = rot("ev", 3)
                        S.op("dve", lambda e, pb=pb, np_=np_, d0=d0, ek=ek: e.tensor_tensor(
                            out=ev[0:np_, ek, :], in0=pb[0:np_, :], in1=C[0:np_, gate_idx, d0:d0 + 512], op=ALU.mult),
                            [pb.b[0], C.b[gate_idx]], [ev.b[ek]])
                        S.op("pool", lambda e, np_=np_, d0=d0, ek=ek, s=s: e.tensor_tensor(
                            out=X[0:np_, xk, s, d0:d0 + 512], in0=X[0:np_, xk, s, d0:d0 + 512], in1=ev[0:np_, ek, :], op=ALU.add),
                            [ev.b[ek], X.b[xk]], [X.b[xk]])

        for cn, n in classes:
            vec = dr["vec_" + cn]
            setup_consts(vec)
            xin = dr["x_" + cn]
            t0 = 0
            while t0 < n:
                nt = min(TT, n - t0)
                subs = [(s, min(128, nt - s * 128)) for s in range((nt + 127) // 128)]
                xk = rot("x", NXB)
                for (s, np_) in subs:
                    S.dma("pool", lambda e, s=s, np_=np_, t0=t0, xk=xk, xin=xin: e.dma_start(
                        out=X[0:np_, xk, s, :], in_=xin[t0 + s * 128:t0 + s * 128 + np_, :]), [], [X.b[xk]], X.b[xk])
                if full:
                    yk = xk
                    yv = dr["y_" + cn].rearrange("(c p) t -> p c t", p=128)
                    S.dma("pool", lambda e, yk=yk, t0=t0, nt=nt, yv=yv: e.dma_start(out=Y[:, yk, :, 0:nt], in_=yv[:, :, t0:t0 + nt]),
                          [], [Y.b[yk]], Y.b[yk])
                    gemm_tm(lambda c, s, np_, yk=yk: Y[:, yk, c, s * 128:s * 128 + np_], Y.b[yk], 16,
                            lambda c, half: wout[c * 128:(c + 1) * 128, half * 1024:(half + 1) * 1024], 0, xk, subs)
                    hfk = rot("hf", 2)
                    for (s, np_) in subs:
                        modnorm(X[0:np_, xk, s, :], X.b[xk], 1, 2, htm[0:np_, s, :], htm.b[s])
                        transpose_to(s, np_, s, hfk)
                    for ffc in range(64):
                        wk = wload(w1t[ffc, :, :], D)
                        pb = nps()
                        for c in range(16):
                            S.op("pe", lambda e, pb=pb, c=c, wk=wk, hfk=hfk, nt=nt: e.matmul(
                                out=pb[:, 0:nt], lhsT=WR[:, wk, c * 128:(c + 1) * 128], rhs=HF[:, hfk, c, 0:nt],
                                start=(c == 0), stop=(c == 15)), [WR.b[wk], HF.b[hfk]], [pb.b[0]])
                        rk = rot("rl", 3)
                        S.op("act", lambda e, pb=pb, rk=rk, nt=nt: e.activation(out=rl[:, rk, 0:nt], in_=pb[:, 0:nt], func=AF.Relu),
                             [pb.b[0]], [rl.b[rk]])
                        eng = "dve" if ffc % 2 == 0 else "pool"
                        S.op(eng, lambda e, rk=rk, ffc=ffc, nt=nt: e.tensor_tensor(
                            out=A[:, ffc, 0:nt], in0=rl[:, rk, 0:nt], in1=rl[:, rk, 0:nt], op=ALU.mult),
                            [rl.b[rk]], [A.b[0]])
                    gemm_tm(lambda c, s, np_: A[:, c, s * 128:s * 128 + np_], A.b[0], 64,
                            lambda c, half: w2[c * 128:(c + 1) * 128, half * 1024:(half + 1) * 1024], 3, xk, subs)
                    if mode == "mid":
                        xo = dr["xo_" + cn]
                        for (s, np_) in subs:
                            S.dma("pool", lambda e, s=s, np_=np_, t0=t0, xk=xk, xo=xo: e.dma_start(
                                out=xo[t0 + s * 128:t0 + s * 128 + np_, :], in_=X[0:np_, xk, s, :]), [X.b[xk]], [], X.b[xk])
                if mode == "last":
                    od = dr["out_" + cn]
                    for (s, np_) in subs:
                        tk = modnorm(X[0:np_, xk, s, :], X.b[xk], 4, 5, None, None)
                        S.dma("pool", lambda e, s=s, np_=np_, t0=t0, tk=tk, od=od: e.dma_start(
                            out=od[t0 + s * 128:t0 + s * 128 + np_, :], in_=tmp[0:np_, tk, :]), [tmp.b[tk]], [], tmp.b[tk])
                else:
                    hfk = rot("hf", 2)
                    for (s, np_) in subs:
                        modnorm(X[0:np_, xk, s, :], X.b[xk], 4, 5, htm[0:np_, s, :], htm.b[s])
                        transpose_to(s, np_, s, hfk)
                    hv = dr["h_" + cn].rearrange("(c p) t -> p c t", p=128)
                    S.dma("pool", lambda e, hfk=hfk, t0=t0, nt=nt, hv=hv: e.dma_start(out=hv[:, :, t0:t0 + nt], in_=HF[:, hfk, :, 0:nt]),
                          [HF.b[hfk]], [], HF.b[hfk])
                t0 += nt
        allb = X.b + HF.b + tmp.b
        S.wait_all("sp", allb)
        S.emit()
    return nc


LT = 482
NCTX = 256
NLAT = 8192
TSEQ = NCTX + NLAT


def seq_tiles(tt, nctx=NCTX, nlat=NLAT):
    tiles = []
    for (s0, s1) in ((0, nctx), (nctx, nctx + nlat)):
        t0 = s0
        while t0 < s1:
            n = min(tt, s1 - t0)
            tiles.append((t0, n, min(2, t0 - s0), min(1, s1 - (t0 + n))))
            t0 += n
    return tiles


def build_mix1(NCTX=NCTX, NLAT=NLAT):
    TSEQ = NCTX + NLAT
    nc = bass.Bass("TRN2", target_bir_lowering=False)
    h_d = nc.dram_tensor("h", [D, TSEQ], BF16, kind="ExternalInput").ap()
    win_d = nc.dram_tensor("win", [D, 1024], BF16, kind="ExternalInput").ap()
    wax_d = nc.dram_tensor("wax", [128, 16 * 256], BF16, kind="ExternalInput").ap()
    pv_d = nc.dram_tensor("pv", [128, 44], F32, kind="ExternalInput").ap()
    y_d = nc.dram_tensor("y", [512, TSEQ], BF16, kind="ExternalOutput").ap()
    scr = {k: nc.dram_tensor("scr_" + k, [512, TSEQ], F32, kind="Internal").ap() for k in ("a1", "u1", "hf", "gg")}
    tiles = seq_tiles(LT, NCTX, NLAT)
    W = LT + 3
    with ExitStack() as st:
        S = Sched(nc, st)
        Win = S.tile("Win", [128, 16, 1024], BF16)
        WA = S.tile("WA", [128, 16, 256], BF16)
        pv = S.tile("pv", [128, 44], F32)
        cl = S.tile("cl", [128, 3, 8], F32)
        H = S.tile("H", [128, 2, 16, W], BF16, 2)
        XB = S.tile("XB", [128, 4, W], F32, 4)
        XS = S.tile("XS", [128, 4, LT], F32, 4)
        XSB = S.tile("XSB", [128, 4, LT], BF16, 4)
        R = S.tile("R", [128, 8, LT], F32, 8)
        I = S.tile("I", [128, 8, LT], F32, 8)
        SQ = S.tile("SQ", [128, 8, LT], F32, 8)
        GG = S.tile("GG", [128, 8, LT], F32, 8)
        HO = S.tile("HO", [128, 8, LT], F32, 8)
        YB = S.tile("YB", [128, 8, LT], BF16, 8)
        stf = S.tile("stf", [128, 8], F32, 8)
        PS = [S.tile(f"PS{i}", [128, 512], F32, 1, psum=True) for i in range(8)]
        psi = [0]

        def nps():
            p = PS[psi[0] % 8]
            psi[0] += 1
            return p
        S.dma("sp", lambda e: e.dma_start(out=Win[:], in_=win_d.rearrange("(c p) n -> p c n", p=128)), [], [Win.b[0]], Win.b[0])
        S.dma("sp", lambda e: e.dma_start(out=WA[:], in_=wax_d.rearrange("p (k n) -> p k n", n=256)), [], [WA.b[0]], WA.b[0])
        S.dma("sp", lambda e: e.dma_start(out=pv[:], in_=pv_d[:, :]), [], [pv.b[0]], pv.b[0])
        S.op("act", lambda e: e.activation(out=cl[:, 0, :], in_=pv[:, 36:44], func=AF.Exp, scale=-1.0), [pv.b[0]], [cl.b[0]])
        S.op("act", lambda e: e.activation(out=cl[:, 1, :], in_=cl[:, 0, :], func=AF.Ln, bias=1.0), [cl.b[0]], [cl.b[0]])
        S.op("dve", lambda e: e.tensor_scalar(out=cl[:, 2, :], in0=cl[:, 1, :], scalar1=-8.0, scalar2=None, op0=ALU.mult),
             [cl.b[0]], [cl.b[0]])
        S.op("dve", lambda e: e.memset(stf[:], 0.0), [], stf.b)

        def pcol(q, oc):
            return pv[:, q * 4 + oc:q * 4 + oc + 1]

        for ti, (t0, n, hl, hr) in enumerate(tiles):
            hk = ti % 2
            nn = hl + n + hr
            S.dma("sp", lambda e, hk=hk, t0=t0, hl=hl, nn=nn: e.dma_start(
                out=H[:, hk, :, 0:nn], in_=h_d.rearrange("(c p) t -> p c t", p=128)[:, :, t0 - hl:t0 - hl + nn]),
                [], [H.b[hk]], H.b[hk])
            off = 2 - hl
            for oc in range(4):
                pb = nps()
                for c in range(16):
                    S.op("pe", lambda e, pb=pb, c=c, oc=oc, hk=hk, nn=nn: e.matmul(
                        out=pb[:, 0:nn], lhsT=Win[:, c, 512 + oc * 128:512 + (oc + 1) * 128], rhs=H[:, hk, c, 0:nn],
                        start=(c == 0), stop=(c == 15)), [Win.b[0], H.b[hk]], [pb.b[0]])
                if hl < 2 or hr < 1:
                    S.op("pool", lambda e, oc=oc: e.memset(XB[:, oc, :], 0.0), [], [XB.b[oc]])
                S.op("act", lambda e, pb=pb, oc=oc, off=off, nn=nn: e.activation(
                    out=XB[:, oc, off:off + nn], in_=pb[:, 0:nn], func=AF.Copy), [pb.b[0]], [XB.b[oc]])
                S.op("act", lambda e, oc=oc, n=n: e.activation(out=XS[:, oc, 0:n], in_=XB[:, oc, 0:n], func=AF.Identity,
                                                               scale=pcol(0, oc), bias=pcol(4, oc)),
                     [XB.b[oc], pv.b[0]], [XS.b[oc]])
                for j in (1, 2, 3):
                    S.op("dve", lambda e, oc=oc, n=n, j=j: e.scalar_tensor_tensor(
                        out=XS[:, oc, 0:n], in0=XB[:, oc, j:j + n], scalar=pcol(j, oc), in1=XS[:, oc, 0:n],
                        op0=ALU.mult, op1=ALU.add), [XB.b[oc], XS.b[oc], pv.b[0]], [XS.b[oc]])
                S.op("pool", lambda e, oc=oc, n=n: e.tensor_copy(out=XSB[:, oc, 0:n], in_=XS[:, oc, 0:n]), [XS.b[oc]], [XSB.b[oc]])
            gk = (ti % 2) * 4
            for oc in range(4):
                pb = nps()
                for c in range(16):
                    S.op("pe", lambda e, pb=pb, c=c, oc=oc, hk=hk, n=n, hl=hl: e.matmul(
                        out=pb[:, 0:n], lhsT=Win[:, c, oc * 128:(oc + 1) * 128], rhs=H[:, hk, c, hl:hl + n],
                        start=(c == 0), stop=(c == 15)), [Win.b[0], H.b[hk]], [pb.b[0]])
                S.op("act", lambda e, pb=pb, oc=oc, n=n, gk=gk: e.activation(out=GG[:, gk + oc, 0:n], in_=pb[:, 0:n], func=AF.Gelu),
                     [pb.b[0]], [GG.b[gk + oc]])
                S.dma("pool", lambda e, oc=oc, n=n, t0=t0, gk=gk: e.dma_start(
                    out=scr["gg"][oc * 128:(oc + 1) * 128, t0:t0 + n], in_=GG[:, gk + oc, 0:n]), [GG.b[gk + oc]], [], GG.b[gk + oc])
            for d in range(2):
                for wi, (dst, bq) in enumerate(((R, 5 + d), (I, 7 + d))):
                    for oc in range(4):
                        blk = oc // 2
                        pb = nps()
                        for kc in range(2):
                            wk = ((d * 2 + wi) * 2 + blk) * 2 + kc
                            S.op("pe", lambda e, pb=pb, wk=wk, oc=oc, blk=blk, kc=kc, n=n: e.matmul(
                                out=pb[:, 0:n], lhsT=WA[:, wk, (oc % 2) * 128:(oc % 2 + 1) * 128], rhs=XSB[:, blk * 2 + kc, 0:n],
                                start=(kc == 0), stop=(kc == 1)), [WA.b[0], XSB.b[blk * 2 + kc]], [pb.b[0]])
                        S.op("act", lambda e, pb=pb, dst=dst, d=d, oc=oc, bq=bq, n=n: e.activation(
                            out=dst[:, d * 4 + oc, 0:n], in_=pb[:, 0:n], func=AF.Sigmoid, bias=pcol(bq, oc)),
                            [pb.b[0], pv.b[0]], [dst.b[d * 4 + oc]])
            for d in range(2):
                for oc in range(4):
                    k = d * 4 + oc
                    S.op("act", lambda e, k=k, n=n: e.activation(out=R[:, k, 0:n], in_=R[:, k, 0:n], func=AF.Exp, scale=cl[:, 2, k:k + 1]),
                         [R.b[k], cl.b[0]], [R.b[k]])
                    S.op("dve", lambda e, k=k, n=n: e.tensor_tensor(out=SQ[:, k, 0:n], in0=R[:, k, 0:n], in1=R[:, k, 0:n], op=ALU.mult),
                         [R.b[k]], [SQ.b[k]])
                    S.op("pool", lambda e, k=k, n=n, oc=oc: e.tensor_tensor(out=I[:, k, 0:n], in0=I[:, k, 0:n], in1=XS[:, oc, 0:n], op=ALU.mult),
                         [I.b[k], XS.b[oc]], [I.b[k]])
            for d in range(2):
                for oc in range(4):
                    k = d * 4 + oc
                    S.op("act", lambda e, k=k, n=n: e.activation(out=SQ[:, k, 0:n], in_=SQ[:, k, 0:n], func=AF.Sqrt, scale=-1.0, bias=1.0),
                         [SQ.b[k]], [SQ.b[k]])
                    S.op("dve", lambda e, k=k, n=n: e.tensor_tensor(out=I[:, k, 0:n], in0=I[:, k, 0:n], in1=SQ[:, k, 0:n], op=ALU.mult),
                         [I.b[k], SQ.b[k]], [I.b[k]])
                    if d == 0:
                        S.op("dve", lambda e, k=k, n=n, oc=oc: e.tensor_tensor_scan(
                            out=HO[:, oc, 0:n], data0=R[:, k, 0:n], data1=I[:, k, 0:n], initial=stf[:, oc:oc + 1],
                            op0=ALU.mult, op1=ALU.add), [R.b[k], I.b[k], stf.b[oc]], [HO.b[oc]])
                        S.op("dve", lambda e, n=n, oc=oc: e.tensor_copy(out=stf[:, oc:oc + 1], in_=HO[:, oc, n - 1:n]), [HO.b[oc]], [stf.b[oc]])
                        S.dma("pool", lambda e, oc=oc, n=n, t0=t0: e.dma_start(
                            out=scr["hf"][oc * 128:(oc + 1) * 128, t0:t0 + n], in_=HO[:, oc, 0:n]), [HO.b[oc]], [], HO.b[oc])
                    else:
                        S.dma("pool", lambda e, oc=oc, n=n, t0=t0, k=k: e.dma_start(
                            out=scr["a1"][oc * 128:(oc + 1) * 128, t0:t0 + n], in_=R[:, k, 0:n]), [R.b[k]], [], R.b[k])
                        S.dma("pool", lambda e, oc=oc, n=n, t0=t0, k=k: e.dma_start(
                            out=scr["u1"][oc * 128:(oc + 1) * 128, t0:t0 + n], in_=I[:, k, 0:n]), [I.b[k]], [], I.b[k])
        S.wait_all("sp", R.b + I.b + HO.b + GG.b)
        S.op("dve", lambda e: e.memset(stf[:, 4:8], 0.0), [], stf.b)
        nctx_tiles = sum(1 for t in tiles if t[0] < NCTX)
        order = list(range(nctx_tiles - 1, -1, -1)) + list(range(len(tiles) - 1, nctx_tiles - 1, -1))
        for pi, ti in enumerate(order):
            (t0, n, hl, hr) = tiles[ti]
            g = (pi % 2) * 4
            for oc in range(4):
                k = g + oc
                for (key, dst) in (("a1", R), ("u1", I), ("hf", SQ), ("gg", GG)):
                    S.dma("sp", lambda e, key=key, dst=dst, oc=oc, k=k, n=n, t0=t0: e.dma_start(
                        out=dst[:, k, 0:n], in_=scr[key][oc * 128:(oc + 1) * 128, t0:t0 + n]), [], [dst.b[k]], dst.b[k])
                S.op("dve", lambda e, k=k, n=n, oc=oc: e.tensor_tensor_scan(
                    out=HO[:, k, n - 1::-1] if n < LT else HO[:, k, ::-1], data0=R[:, k, n - 1::-1] if n < LT else R[:, k, ::-1],
                    data1=I[:, k, n - 1::-1] if n < LT else I[:, k, ::-1], initial=stf[:, 4 + oc:5 + oc],
                    op0=ALU.mult, op1=ALU.add), [R.b[k], I.b[k], stf.b[4 + oc]], [HO.b[k]])
                S.op("dve", lambda e, k=k, oc=oc: e.tensor_copy(out=stf[:, 4 + oc:5 + oc], in_=HO[:, k, 0:1]), [HO.b[k]], [stf.b[4 + oc]])
                S.op("pool", lambda e, k=k, n=n: e.tensor_tensor(out=HO[:, k, 0:n], in0=HO[:, k, 0:n], in1=SQ[:, k, 0:n], op=ALU.add),
                     [HO.b[k], SQ.b[k]], [HO.b[k]])
                S.op("dve", lambda e, k=k, n=n: e.tensor_tensor(out=YB[:, k, 0:n], in0=HO[:, k, 0:n], in1=GG[:, k, 0:n], op=ALU.mult),
                     [HO.b[k], GG.b[k]], [YB.b[k]])
                S.dma("pool", lambda e, oc=oc, k=k, n=n, t0=t0: e.dma_start(
                    out=y_d[oc * 128:(oc + 1) * 128, t0:t0 + n], in_=YB[:, k, 0:n]), [YB.b[k]], [], YB.b[k])
        S.wait_all("sp", YB.b)
        S.emit()
    return nc


CH = 64
MT = 256
WCOLS = 2208
C_R, C_K, C_V, C_GD, C_WD, C_AD, C_GQ, C_GK, C_GV, C_GG, C_GGD = 0, 256, 512, 768, 1024, 1216, 1408, 1536, 1664, 1920, 2176
LDS = 0.6065306597126334


class Ops:
    def __init__(self, S):
        self.S = S

    def tt(self, eng, out, in0, in1, op, R, W):
        self.S.op(eng, lambda e: e.tensor_tensor(out=out, in0=in0, in1=in1, op=op), R, W)

    def ts(self, eng, out, in0, s1, s2, op0, op1, R, W):
        if op1 is None:
            self.S.op(eng, lambda e: e.tensor_scalar(out=out, in0=in0, scalar1=s1, scalar2=None, op0=op0), R, W)
        else:
            self.S.op(eng, lambda e: e.tensor_scalar(out=out, in0=in0, scalar1=s1, scalar2=s2, op0=op0, op1=op1), R, W)

    def stt(self, out, in0, scalar, in1, op0, op1, R, W):
        self.S.op("dve", lambda e: e.scalar_tensor_tensor(out=out, in0=in0, scalar=scalar, in1=in1, op0=op0, op1=op1), R, W)

    def act(self, out, in_, func, R, W, scale=None, bias=None, accum=None):
        kw = {}
        if scale is not None:
            kw["scale"] = scale
        if bias is not None:
            kw["bias"] = bias
        if accum is not None:
            kw["accum_out"] = accum
        self.S.op("act", lambda e: e.activation(out=out, in_=in_, func=func, **kw), R, W)

    def cp(self, eng, out, in_, R, W):
        if eng == "act":
            self.S.op("act", lambda e: e.activation(out=out, in_=in_, func=AF.Copy), R, W)
        else:
            self.S.op(eng, lambda e: e.tensor_copy(out=out, in_=in_), R, W)

    def mm(self, out, lhsT, rhs, start, stop, R, W):
        self.S.op("pe", lambda e: e.matmul(out=out, lhsT=lhsT, rhs=rhs, start=start, stop=stop), R, W)

    def tr(self, out, in_, ident, R, W):
        self.S.op("pe", lambda e: e.transpose(out=out, in_=in_, identity=ident), R, W)

    def scan(self, out, d0, d1, init, R, W):
        self.S.op("dve", lambda e: e.tensor_tensor_scan(out=out, data0=d0, data1=d1, initial=init, op0=ALU.mult, op1=ALU.add), R, W)

    def memset(self, eng, ap, val, W):
        self.S.op(eng, lambda e: e.memset(ap, val), [], W)

    def dma(self, q, out, in_, R, W, anchor):
        self.S.dma(q, lambda e: e.dma_start(out=out, in_=in_), R, W, anchor)


class _Stop(Exception):
    pass


def build_mix0(NCTX=NCTX, NLAT=NLAT, dbg=False):
    TSEQ = NCTX + NLAT
    nc = bass.Bass("TRN2", target_bir_lowering=False)
    h_d = nc.dram_tensor("h", [D, TSEQ], BF16, kind="ExternalInput").ap()
    win_d = nc.dram_tensor("win", [D, WCOLS], BF16, kind="ExternalInput").ap()
    g2_d = nc.dram_tensor("g2", [128, 2 * 256], BF16, kind="ExternalInput").ap()
    w2_d = nc.dram_tensor("w2", [96, 4 * 256], BF16, kind="ExternalInput").ap()
    gw2_d = nc.dram_tensor("gw2", [16, 2 * 128], BF16, kind="ExternalInput").ap()
    pr_d = nc.dram_tensor("pr", [64, 64], F32, kind="ExternalInput").ap()
    pg_d = nc.dram_tensor("pg", [128, 16], F32, kind="ExternalInput").ap()
    cst_d = nc.dram_tensor("cst", [128, 1280], F32, kind="ExternalInput").ap()
    identb_d = nc.dram_tensor("identb", [128, 128], BF16, kind="ExternalInput").ap()
    y_d = nc.dram_tensor("y", [512, TSEQ], BF16, kind="ExternalOutput").ap()
    scr_y = nc.dram_tensor("scr_y", [TSEQ, 512], F32, kind="Internal").ap()
    scr_bn = nc.dram_tensor("scr_bn", [4, 64, TSEQ], F32, kind="Internal").ap()
    ntiles_ctx = NCTX // MT
    tiles = [(t0, MT) for t0 in range(0, TSEQ, MT)]
    assert NCTX % MT == 0 and NLAT % MT == 0
    with ExitStack() as st:
        S = Sched(nc, st)
        O = Ops(S)
        Win = S.tile("Win", [128, 16, WCOLS], BF16)
        G2 = S.tile("G2", [128, 2, 256], BF16)
        W2 = S.tile("W2", [96, 4, 256], BF16)
        GW2 = S.tile("GW2", [16, 2, 128], BF16)
        PR = S.tile("PR", [64, 96], F32)
        PG = S.tile("PG", [128, 24], F32)
        CST = S.tile("CST", [128, 1280], F32)
        IDB = S.tile("IDB", [128, 128], BF16)
        ONES = S.tile("ONES", [64, 64], BF16)
        for (t, src) in ((Win, win_d.rearrange("(c p) n -> p c n", p=128)), (G2, g2_d.rearrange("p (k n) -> p k n", n=256)),
                         (W2, w2_d.rearrange("p (k n) -> p k n", n=256)), (GW2, gw2_d.rearrange("p (k n) -> p k n", n=128)),
                         (CST, cst_d[:, :]), (IDB, identb_d[:, :])):
            O.dma("sp", t[:], src, [], [t.b[0]], t.b[0])
        O.dma("sp", PR[:, 0:64], pr_d[:, :], [], [PR.b[0]], PR.b[0])
        O.dma("sp", PG[:, 0:16], pg_d[:, :], [], [PG.b[0]], PG.b[0])
        IDF = CST[:, 0:128]

        def cmask(kind):
            return CST[0:64, 128 + kind * 64:128 + (kind + 1) * 64]
        SMF = CST[:, 384:640]
        SMB = CST[:, 640:896]
        O.memset("pool", ONES[:], 1.0, [ONES.b[0]])
        mu_v = PR[:, 0:24].rearrange("p (g m) -> p g m", m=2)
        O.tt("dve", PR[:, 64:76], mu_v[:, :, 0], mu_v[:, :, 1], ALU.add, [PR.b[0]], [PR.b[0]])
        O.ts("dve", PR[:, 64:76], PR[:, 64:76], -1.0, 1.0, ALU.mult, ALU.add, [PR.b[0]], [PR.b[0]])
        O.ts("dve", PR[:, 76:80], PR[:, 28:32], -1.0, 1.0, ALU.mult, ALU.add, [PR.b[0]], [PR.b[0]])
        for (dst, a, b_) in ((16, 0, 1), (17, 2, 3), (18, 4, 5), (19, 6, 7), (20, 8, 9), (21, 10, 11)):
            O.tt("dve", PG[:, dst:dst + 1], PG[:, a:a + 1], PG[:, b_:b_ + 1], ALU.add, [PG.b[0]], [PG.b[0]])
        O.ts("dve", PG[:, 16:22], PG[:, 16:22], -1.0, 1.0, ALU.mult, ALU.add, [PG.b[0]], [PG.b[0]])
        O.ts("dve", PG[:, 22:24], PG[:, 12:14], -1.0, None, ALU.mult, None, [PG.b[0]], [PG.b[0]])

        def prc(c):
            return PR[:, c:c + 1]
        NH = MT + 2
        H = S.tile("H", [128, 2, 16, NH], BF16, 2)
        RAW = S.tile("RAW", [128, 2, NH], F32, 2)
        T1 = S.tile("T1", [128, 2, MT], F32, 2)
        fm = {}
        for nm in ("r", "k", "v", "kk", "a", "sg", "e1", "e2", "kd", "bb", "g", "bon", "t2"):
            fm[nm] = S.tile("f_" + nm, [64, 4, MT], F32)
        fm["cs"] = fm["k"]
        fm["e3"] = fm["a"]
        GD = S.tile("GD", [128, 2, MT], BF16)
        LW = S.tile("LW", [96, MT], BF16)
        LA = S.tile("LA", [96, MT], BF16)
        AR = S.tile("AR", [64, 4, 4, 128], BF16)
        BK = S.tile("BK", [64, 4, 4, 128], BF16)
        RZ = S.tile("RZ", [64, 4, 4, 128], BF16)
        VT = S.tile("VT", [64, 4, 4, 64], BF16)
        WC = S.tile("WC", [64, 4, 4], F32)
        ST = S.tile("ST", [64, 4, 64], F32)
        STB = S.tile("STB", [64, 4, 64], BF16)
        X = S.tile("X", [64, 2, 4, 128], BF16, 2)
        ZK = S.tile("ZK", [64, 4, 128], F32)
        TRs = S.tile("TRs", [64, 4, 64], BF16)
        Pm = S.tile("Pm", [64, 2, 4, 128], F32, 2)
        TT = S.tile("TT", [64, 2, 4, 64], F32, 2)
        TTB = S.tile("TTB", [64, 4, 64], BF16)
        G2s = S.tile("G2s", [64, 4, 64], BF16)
        QA = S.tile("QA", [64, 4, 128], BF16)
        HH = S.tile("HH", [64, 4, 128], BF16)
        TMP = S.tile("TMP", [64, 4, 64], F32)
        YT = S.tile("YT", [64, 2, 512], F32, 2)
        YF = S.tile("YF", [64, 1, 512], F32, 1)
        gl = {}
        for nm in ("q", "k", "sp", "cs", "e1", "e2", "e3"):
            gl[nm] = S.tile("g_" + nm, [128, MT], F32)
        GQ = S.tile("GQ", [128, MT], BF16)
        GK = S.tile("GK", [128, MT], BF16)
        GKE = S.tile("GKE", [128, MT], BF16)
        GGD = S.tile("GGD", [16, MT], BF16)
        SG = S.tile("SG", [128, 2, MT], F32)
        VG = S.tile("VG", [64, 4, 256], BF16)
        GS = S.tile("GS", [128, 256], F32)
        GSB = S.tile("GSB", [128, 256], BF16)
        SC = S.tile("SC", [64, 64], BF16)
        KE = S.tile("KE", [64, 128], BF16)
        DEC = S.tile("DEC", [128, 4], F32)
        stat = S.tile("stat", [64, 16], F32)
        ON = S.tile("ON", [64, 512], F32)
        YO = S.tile("YO", [128, 4, CH], BF16, 1)
        BF = fm["t2"]
        PSB = S.tile("PSB", [128, 1024], BF16, 1, psum=True)
        PSL = [S.tile(f"PS{i}", [128, 512], F32, 1, psum=True) for i in range(7)]
        psi = [0]

        def nps():
            p = PSL[psi[0] % 7]
            psi[0] += 1
            return p
        hv = h_d.rearrange("(c p) t -> p c t", p=128)

        def proj(cols, ncols, hk, lo, nn, pb, nparts=None):
            for c in range(16):
                O.mm(pb[0:ncols, 0:nn], Win[:, c, cols:cols + ncols], H[:, hk, c, lo:lo + nn], c == 0, c == 15,
                     [Win.b[0], H.b[hk]], [pb.b[0]])

        def mix_into(pb, npart, nn, hl, rk, c0, m0, m1, out_ap, out_bufs):
            off = 1 - hl
            if hl == 0 or nn < hl + MT + 1:
                O.memset("pool", RAW[0:npart, rk, :], 0.0, [RAW.b[rk]])
            O.cp("act", RAW[0:npart, rk, off:off + nn], pb[0:npart, 0:nn], [pb.b[0]], [RAW.b[rk]])
            O.ts("dve", T1[0:npart, rk, :], RAW[0:npart, rk, 1:1 + MT], c0, None, ALU.mult, None, [RAW.b[rk], PR.b[0], PG.b[0]], [T1.b[rk]])
            O.stt(T1[0:npart, rk, :], RAW[0:npart, rk, 0:MT], m0, T1[0:npart, rk, :], ALU.mult, ALU.add,
                  [RAW.b[rk], T1.b[rk], PR.b[0], PG.b[0]], [T1.b[rk]])
            O.stt(out_ap, RAW[0:npart, rk, 2:2 + MT], m1, T1[0:npart, rk, :], ALU.mult, ALU.add,
                  [RAW.b[rk], T1.b[rk], PR.b[0], PG.b[0]], out_bufs)

        def run_pass(d):
            strict_g1, incl_g1, strict_g3 = ((0, 1, 2) if d == 0 else (2, 3, 0))
            SM = SMF if d == 0 else SMB
            O.memset("dve", ST[:], 0.0, [ST.b[0]])
            O.memset("dve", STB[:], 0.0, [STB.b[0]])
            O.memset("pool", GS[:], 0.0, [GS.b[0]])
            O.memset("pool", GSB[:], 0.0, [GSB.b[0]])
            O.memset("pool", RZ[:], 0.0, [RZ.b[0]])
            if d == 0:
                order = list(range(len(tiles)))
            else:
                order = list(range(ntiles_ctx - 1, -1, -1)) + list(range(len(tiles) - 1, ntiles_ctx - 1, -1))
            for oi, ti in enumerate(order):
                t0, n = tiles[ti]
                seg0, seg1 = (0, NCTX) if t0 < NCTX else (NCTX, TSEQ)
                hl = 1 if t0 > seg0 else 0
                hr = 1 if t0 + n < seg1 else 0
                nn = hl + n + hr
                hk = oi % 2
                O.dma("sp", H[:, hk, :, 0:nn], hv[:, :, t0 - hl:t0 - hl + nn], [], [H.b[hk]], H.b[hk])
                rki = [0]

                def nrk():
                    rki[0] += 1
                    return rki[0] % 2
                for q, nm in enumerate(("r", "k", "v")):
                    for j in range(4):
                        pb = nps()
                        proj((C_R, C_K, C_V)[q] + j * 64, 64, hk, 0, nn, pb)
                        g = q * 4 + j
                        mix_into(pb, 64, nn, hl, nrk(), prc(64 + g), prc(g * 2), prc(g * 2 + 1), fm[nm][:, j, :], [fm[nm].b[0]])
                for kc in range(2):
                    pb = nps()
                    proj(C_GD + kc * 128, 128, hk, 0, nn, pb)
                    rk = nrk()
                    mix_into(pb, 128, nn, hl, rk, PG[:, 16 + kc:17 + kc], PG[:, kc * 2:kc * 2 + 1], PG[:, kc * 2 + 1:kc * 2 + 2],
                             T1[:, rk, :], [T1.b[rk]])
                    O.act(GD[:, kc, :], T1[:, rk, :], AF.Sigmoid, [T1.b[rk]], [GD.b[0]])
                for li, (dst, col, fn) in enumerate(((LW, C_WD + d * 96, AF.Tanh), (LA, C_AD + d * 96, AF.Copy))):
                    pb = nps()
                    proj(col, 96, hk, 0, nn, pb)
                    rk = nrk()
                    grp = li * 2 + d
                    mix_into(pb, 96, nn, hl, rk, PG[0:96, 18 + grp:19 + grp], PG[0:96, 4 + grp * 2:5 + grp * 2],
                             PG[0:96, 5 + grp * 2:6 + grp * 2], T1[0:96, rk, :], [T1.b[rk]])
                    O.act(dst[:], T1[0:96, rk, :], fn, [T1.b[rk]], [dst.b[0]])
                for j in range(4):
                    pb = nps()
                    for kc in range(2):
                        O.mm(pb[0:64, 0:MT], G2[:, kc, j * 64:(j + 1) * 64], GD[:, kc, :], kc == 0, kc == 1, [G2.b[0], GD.b[0]], [pb.b[0]])
                    O.cp("act", fm["g"][:, j, :], pb[0:64, 0:MT], [pb.b[0]], [fm["g"].b[0]])
                    pb = nps()
                    O.mm(pb[0:64, 0:MT], W2[:, d, j * 64:(j + 1) * 64], LW[:], True, True, [W2.b[0], LW.b[0]], [pb.b[0]])
                    O.act(fm["sg"][:, j, :], pb[0:64, 0:MT], AF.Sigmoid, [pb.b[0], PR.b[0]], [fm["sg"].b[0]], bias=prc(36 + d * 4 + j))
                    pb = nps()
                    O.mm(pb[0:64, 0:MT], W2[:, 2 + d, j * 64:(j + 1) * 64], LA[:], True, True, [W2.b[0], LA.b[0]], [pb.b[0]])
                    O.act(fm["a"][:, j, :], pb[0:64, 0:MT], AF.Sigmoid, [pb.b[0], PR.b[0]], [fm["a"].b[0]], bias=prc(44 + d * 4 + j))
                    O.ts("dve", fm["kk"][:, j, :], fm["k"][:, j, :], prc(24 + j), None, ALU.mult, None, [fm["k"].b[0], PR.b[0]], [fm["kk"].b[0]])
                SQ = fm["t2"]
                O.tt("pool", SQ[:], fm["kk"][:], fm["kk"][:], ALU.mult, [fm["kk"].b[0]], [SQ.b[0]])
                sqb = AR[:].rearrange("p a b c -> p (a b c)")[:, 0:4 * MT].rearrange("p (j t) -> p j t", t=MT)
                O.cp("pool", sqb, SQ[:], [SQ.b[0]], [AR.b[0]])
                for j in range(4):
                    pb = nps()
                    O.mm(pb[0:64, 0:MT], ONES[:], sqb[:, j, :], True, True, [ONES.b[0], AR.b[0]], [pb.b[0]])
                    O.act(SQ[:, j, :], pb[0:64, 0:MT], AF.Sqrt, [pb.b[0]], [SQ.b[0]])
                O.ts("dve", SQ[:], SQ[:], 1e-12, None, ALU.max, None, [SQ.b[0]], [SQ.b[0]])
                S.op("dve", lambda e, o=SQ[:]: e.reciprocal(out=o, in_=o), [SQ.b[0]], [SQ.b[0]])
                O.tt("dve", fm["kk"][:], fm["kk"][:], SQ[:], ALU.mult, [fm["kk"].b[0], SQ.b[0]], [fm["kk"].b[0]])
                for j in range(4):
                    O.ts("dve", fm["kd"][:, j, :], fm["a"][:, j, :], prc(28 + j), prc(76 + j), ALU.mult, ALU.add,
                         [fm["a"].b[0], PR.b[0]], [fm["kd"].b[0]])
                O.tt("pool", fm["kd"][:], fm["kd"][:], fm["k"][:], ALU.mult, [fm["kd"].b[0], fm["k"].b[0]], [fm["kd"].b[0]])
                O.tt("pool", fm["bb"][:], fm["kk"][:], fm["a"][:], ALU.mult, [fm["kk"].b[0], fm["a"].b[0]], [fm["bb"].b[0]])
                O.tt("dve", SQ[:], fm["r"][:], fm["kd"][:], ALU.mult, [fm["r"].b[0], fm["kd"].b[0]], [SQ.b[0]])
                for j in range(4):
                    O.ts("dve", sqb[:, j, :], SQ[:, j, :], prc(32 + j), None, ALU.mult, None, [SQ.b[0], PR.b[0]], [AR.b[0]])
                for j in range(4):
                    pb = nps()
                    O.mm(pb[0:64, 0:MT], ONES[:], sqb[:, j, :], True, True, [ONES.b[0], AR.b[0]], [pb.b[0]])
                    O.tt("dve", fm["bon"][:, j, :], pb[0:64, 0:MT], fm["v"][:, j, :], ALU.mult, [pb.b[0], fm["v"].b[0]], [fm["bon"].b[0]])
                if d == 0:
                    O.dma("pool", scr_bn[:, :, t0:t0 + n].rearrange("j p t -> p j t"), fm["bon"][:], [fm["bon"].b[0]], [], fm["bon"].b[0])
                else:
                    O.dma("sp", BF[:], scr_bn[:, :, t0:t0 + n].rearrange("j p t -> p j t"), [], [BF.b[0]], BF.b[0])
                    O.tt("pool", fm["bon"][:], fm["bon"][:], BF[:], ALU.add, [fm["bon"].b[0], BF.b[0]], [fm["bon"].b[0]])
                sgf = fm["sg"][:].rearrange("p j t -> p (j t)")
                csf = fm["cs"][:].rearrange("p j t -> p (j t)")
                for j in range(4):
                    if d == 0:
                        O.scan(fm["cs"][:, j, :], SM[0:64, :], fm["sg"][:, j, :], 0.0, [fm["sg"].b[0], CST.b[0]], [fm["cs"].b[0]])
                    else:
                        O.scan(fm["cs"][:, j, ::-1], SM[0:64, ::-1], fm["sg"][:, j, ::-1], 0.0, [fm["sg"].b[0], CST.b[0]], [fm["cs"].b[0]])
                O.act(fm["e1"][:], fm["cs"][:], AF.Exp, [fm["cs"].b[0]], [fm["e1"].b[0]], scale=-LDS)
                O.tt("pool", fm["e2"][:], fm["cs"][:], fm["sg"][:], ALU.subtract, [fm["cs"].b[0], fm["sg"].b[0]], [fm["e2"].b[0]])
                O.act(fm["e2"][:], fm["e2"][:], AF.Exp, [fm["e2"].b[0]], [fm["e2"].b[0]], scale=-LDS)
                O.act(fm["e3"][:], fm["cs"][:], AF.Exp, [fm["cs"].b[0]], [fm["e3"].b[0]], scale=LDS)
                e1v = fm["e1"][:].rearrange("p j (c t) -> p c j t", t=CH)
                endp = CH - 1 if d == 0 else 0
                O.cp("pool", WC[:], e1v[:, :, :, endp], [fm["e1"].b[0]], [WC.b[0]])

                def chunked(ap):
                    return ap.rearrange("p j (c t) -> p j c t", t=CH)
                vbf = BK[:].rearrange("p a b c -> p (a b c)")[:, 0:4 * MT].rearrange("p (j t) -> p j t", t=MT)
                O.cp("act", vbf, fm["v"][:], [fm["v"].b[0]], [BK.b[0]])
                for c in range(4):
                    for j in range(4):
                        O.tr(PSB[0:64, (c * 4 + j) * 64:(c * 4 + j + 1) * 64], vbf[:, j, c * CH:(c + 1) * CH], IDB[0:64, 0:64],
                             [BK.b[0], IDB.b[0]], [PSB.b[0]])
                O.cp("act", VT[:].rearrange("p c j v -> p (c j v)"), PSB[0:64, 0:1024], [PSB.b[0]], [VT.b[0]])
                O.stt(AR[:, :, :, 0:64], chunked(fm["kk"][:]), -1.0, chunked(fm["e2"][:]), ALU.mult, ALU.mult,
                      [fm["kk"].b[0], fm["e2"].b[0], AR.b[0]], [AR.b[0]])
                O.tt("pool", fm["t2"][:], fm["r"][:], fm["e1"][:], ALU.mult, [fm["r"].b[0], fm["e1"].b[0], SQ.b[0]], [fm["t2"].b[0]])
                O.cp("act", AR[:, :, :, 64:128], chunked(fm["t2"][:]), [fm["t2"].b[0]], [AR.b[0]])
                O.cp("pool", RZ[:, :, :, 64:128].rearrange("p c j t -> p j c t"), chunked(fm["t2"][:]), [fm["t2"].b[0]], [RZ.b[0]])
                O.tt("dve", BK[:, :, :, 0:64], chunked(fm["bb"][:]), chunked(fm["e3"][:]), ALU.mult, [fm["bb"].b[0], fm["e3"].b[0]], [BK.b[0]])
                O.tt("dve", BK[:, :, :, 64:128], chunked(fm["kd"][:]), chunked(fm["e3"][:]), ALU.mult, [fm["kd"].b[0], fm["e3"].b[0]], [BK.b[0]])
                gla_prep(d, hk, hl, n, t0)
                corder = range(4) if d == 0 else range(3, -1, -1)
                for ci, c in enumerate(corder):
                    rwkv_chunk(d, c, strict_g1, incl_g1, strict_g3, ci)
                    if dbg:
                        raise _Stop()
                    gla_chunk(d, c, ci)
                    finish_chunk(d, c, t0 + c * CH, ci)


        def rwkv_chunk(d, c, k_s1, k_i1, k_s3, ci):
            xk = 0
            g1, g2p, g3, trp = nps(), nps(), nps(), PSB
            for u in range(4):
                O.mm(g1[0:64, u * 128:(u + 1) * 128], BK[:, u, c, 0:64], AR[:, u, c, :], True, True, [BK.b[0], AR.b[0]], [g1.b[0]])
                O.mm(g2p[0:64, u * 128:(u + 1) * 128], BK[:, u, c, 64:128], AR[:, u, c, :], True, True, [BK.b[0], AR.b[0]], [g2p.b[0]])
                O.mm(g3[0:64, u * 128:(u + 1) * 128], AR[:, u, c, 0:64], BK[:, u, c, :], True, True, [BK.b[0], AR.b[0]], [g3.b[0]])
                O.tr(trp[0:64, u * 192:u * 192 + 64], BK[:, u, c, 0:64], IDB[0:64, 0:64], [BK.b[0], IDB.b[0]], [PSB.b[0]])
                O.tr(trp[0:64, u * 192 + 64:u * 192 + 128], BK[:, u, c, 64:128], IDB[0:64, 0:64], [BK.b[0], IDB.b[0]], [PSB.b[0]])
                O.tr(trp[0:64, u * 192 + 128:u * 192 + 192], AR[:, u, c, 0:64], IDB[0:64, 0:64], [AR.b[0], IDB.b[0]], [PSB.b[0]])
            g1v = g1[0:64, :].rearrange("p (u x) -> p u x", x=128)
            g2v = g2p[0:64, :].rearrange("p (u x) -> p u x", x=128)
            g3v = g3[0:64, :].rearrange("p (u x) -> p u x", x=128)
            trv = trp[0:64, 0:768].rearrange("p (u x) -> p u x", x=192)

            def m4(kind):
                return cmask(kind).unsqueeze(1).to_broadcast([64, 4, 64])
            O.tt("dve", Pm[:, 0, :, 0:64], g1v[:, :, 0:64], m4(k_s1), ALU.mult, [g1.b[0], CST.b[0]], [Pm.b[0]])
            O.tt("dve", X[:, 0, :, 64:128], g1v[:, :, 64:128], m4(k_i1), ALU.mult, [g1.b[0], CST.b[0]], [X.b[0]])
            O.cp("act", X[:, 0, :, 0:64], trv[:, :, 0:64], [PSB.b[0]], [X.b[0]])
            O.cp("act", ZK[:, :, 0:64], trv[:, :, 64:128], [PSB.b[0]], [ZK.b[0]])
            O.tt("dve", ZK[:, :, 64:128], g2v[:, :, 64:128], m4(k_i1), ALU.mult, [g2p.b[0], CST.b[0]], [ZK.b[0]])
            O.cp("act", TRs[:], trv[:, :, 128:192], [PSB.b[0]], [TRs.b[0]])
            O.tt("dve", Pm[:, 0, :, 64:128], g3v[:, :, 0:64], m4(k_s3), ALU.mult, [g3.b[0], CST.b[0]], [Pm.b[0]])
            O.tt("dve", G2s[:], g3v[:, :, 64:128], m4(k_s3), ALU.mult, [g3.b[0], CST.b[0]], [G2s.b[0]])
            idb = IDF[0:64, 0:64].unsqueeze(1).to_broadcast([64, 4, 64])
            O.tt("pool", TT[:, 0], Pm[:, 0, :, 64:128], idb, ALU.add, [Pm.b[0], CST.b[0]], [TT.b[0]])
            pk, tk = 0, 0
            for lvl in range(1, 6):
                pp = nps()
                for u in range(4):
                    O.mm(pp[0:64, u * 128:u * 128 + 64], Pm[:, pk, u, 64:128], Pm[:, pk, u, 0:64], True, True, [Pm.b[pk]], [pp.b[0]])
                    O.mm(pp[0:64, u * 128 + 64:u * 128 + 128], Pm[:, pk, u, 0:64], Pm[:, pk, u, 64:128], True, True, [Pm.b[pk]], [pp.b[0]])
                O.cp("act", Pm[:, 1 - pk].rearrange("p u x -> p (u x)"), pp[0:64, :], [pp.b[0]], [Pm.b[1 - pk]])
                pk = 1 - pk
                tp_ = nps()
                for u in range(4):
                    O.mm(tp_[0:64, u * 64:(u + 1) * 64], Pm[:, pk, u, 0:64], TT[:, tk, u, :], True, True, [Pm.b[pk], TT.b[tk]], [tp_.b[0]])
                O.tt("dve", TT[:, 1 - tk], tp_[0:64, 0:256].rearrange("p (u x) -> p u x", x=64), TT[:, tk], ALU.add,
                     [tp_.b[0], TT.b[tk]], [TT.b[1 - tk]])
                tk = 1 - tk
            O.cp("pool", TTB[:], TT[:, tk], [TT.b[tk]], [TTB.b[0]])
            xp = nps()
            for u in range(4):
                O.mm(xp[0:64, u * 128:(u + 1) * 128], TTB[:, u, :], X[:, 0, u, :], True, True, [TTB.b[0], X.b[0]], [xp.b[0]])
            O.cp("dve", X[:, 1].rearrange("p u x -> p (u x)"), xp[0:64, :], [xp.b[0]], [X.b[1]])
            xk = 1
            qa, hh = nps(), nps()
            for u in range(4):
                O.mm(qa[0:64, u * 128:(u + 1) * 128], TRs[:, u, :], X[:, xk, u, :], True, True, [TRs.b[0], X.b[xk]], [qa.b[0]])
                O.mm(hh[0:64, u * 128:(u + 1) * 128], G2s[:, u, :], X[:, xk, u, :], True, True, [G2s.b[0], X.b[xk]], [hh.b[0]])
            O.tt("dve", QA[:], qa[0:64, :].rearrange("p (u x) -> p u x", x=128), RZ[:, c, :, :], ALU.add, [qa.b[0], RZ.b[0]], [QA.b[0]])
            O.tt("dve", HH[:], hh[0:64, :].rearrange("p (u x) -> p u x", x=128), ZK[:], ALU.add, [hh.b[0], ZK.b[0]], [HH.b[0]])
            sp_, yp = nps(), nps()
            for u in range(4):
                O.mm(sp_[0:64, u * 64:(u + 1) * 64], QA[:, u, 0:64], STB[:, u, :], True, False, [QA.b[0], STB.b[0]], [sp_.b[0]])
                O.mm(sp_[0:64, u * 64:(u + 1) * 64], HH[:, u, 0:64], VT[:, c, u, :], False, True, [HH.b[0], VT.b[0]], [sp_.b[0]])
                O.mm(yp[0:64, u * 64:(u + 1) * 64], QA[:, u, 64:128], STB[:, u, :], True, False, [QA.b[0], STB.b[0]], [yp.b[0]])
                O.mm(yp[0:64, u * 64:(u + 1) * 64], HH[:, u, 64:128], VT[:, c, u, :], False, True, [HH.b[0], VT.b[0]], [yp.b[0]])
            yk = ci % 2
            O.cp("act", YT[:, yk, 0:256], yp[0:64, 0:256], [yp.b[0]], [YT.b[yk]])
            O.tt("dve", TMP[:], sp_[0:64, 0:256].rearrange("p (u v) -> p u v", v=64), ST[:], ALU.add, [sp_.b[0], ST.b[0]], [TMP.b[0]])
            O.tt("dve", ST[:], TMP[:], WC[:, c, :].unsqueeze(2).to_broadcast([64, 4, 64]), ALU.mult, [TMP.b[0], WC.b[0]], [ST.b[0]])
            O.cp("pool", STB[:], ST[:], [ST.b[0]], [STB.b[0]])

        def gla_prep(d, hk, hl, n, t0):
            SM = SMF if d == 0 else SMB
            for (nm, col) in (("q", C_GQ), ("k", C_GK)):
                pb = nps()
                proj(col, 128, hk, hl, n, pb)
                O.cp("act", gl[nm][:], pb[:, 0:n], [pb.b[0]], [gl[nm].b[0]])
            pb = nps()
            proj(C_GGD + d * 16, 16, hk, hl, n, pb)
            O.cp("act", GGD[:], pb[0:16, 0:n], [pb.b[0]], [GGD.b[0]])
            for c2 in range(2):
                pb = nps()
                proj(C_GG + c2 * 128, 128, hk, hl, n, pb)
                O.act(SG[:, c2, :], pb[:, 0:n], AF.Silu, [pb.b[0]], [SG.b[0]])
            for c in range(4):
                pb = nps()
                for cc in range(16):
                    O.mm(pb[0:64, 0:256], H[:, hk, cc, hl + c * CH:hl + (c + 1) * CH], Win[:, cc, C_GV:C_GV + 256], cc == 0, cc == 15,
                         [Win.b[0], H.b[hk]], [pb.b[0]])
                O.cp("act", VG[:, c, :], pb[0:64, 0:256], [pb.b[0]], [VG.b[0]])
            pb = nps()
            O.mm(pb[:, 0:n], GW2[:, d, :], GGD[:], True, True, [GW2.b[0], GGD.b[0]], [pb.b[0]])
            O.act(gl["sp"][:], pb[:, 0:n], AF.Exp, [pb.b[0], PG.b[0]], [gl["sp"].b[0]], scale=-1.0, bias=PG[:, 22 + d:23 + d])
            O.act(gl["sp"][:], gl["sp"][:], AF.Ln, [gl["sp"].b[0]], [gl["sp"].b[0]], bias=1.0)
            if d == 0:
                O.scan(gl["cs"][:], SM, gl["sp"][:], 0.0, [gl["sp"].b[0], CST.b[0]], [gl["cs"].b[0]])
            else:
                O.scan(gl["cs"][:, ::-1], SM[:, ::-1], gl["sp"][:, ::-1], 0.0, [gl["sp"].b[0], CST.b[0]], [gl["cs"].b[0]])
            O.act(gl["e1"][:], gl["cs"][:], AF.Exp, [gl["cs"].b[0]], [gl["e1"].b[0]], scale=-1.0 / 16)
            O.act(gl["e2"][:], gl["cs"][:], AF.Exp, [gl["cs"].b[0]], [gl["e2"].b[0]], scale=1.0 / 16)
            endp = CH - 1 if d == 0 else 0
            csv = gl["cs"][:].rearrange("p (c t) -> p c t", t=CH)
            O.tt("pool", gl["e3"][:].rearrange("p (c t) -> p c t", t=CH), csv, csv[:, :, endp:endp + 1].to_broadcast([128, 4, CH]),
                 ALU.subtract, [gl["cs"].b[0]], [gl["e3"].b[0]])
            O.act(gl["e3"][:], gl["e3"][:], AF.Exp, [gl["e3"].b[0]], [gl["e3"].b[0]], scale=1.0 / 16)
            O.cp("pool", DEC[:], gl["e1"][:].rearrange("p (c t) -> p c t", t=CH)[:, :, endp], [gl["e1"].b[0]], [DEC.b[0]])
            O.stt(GQ[:], gl["q"][:], 128.0 ** -0.5, gl["e1"][:], ALU.mult, ALU.mult, [gl["q"].b[0], gl["e1"].b[0]], [GQ.b[0]])
            O.tt("pool", GK[:], gl["k"][:], gl["e2"][:], ALU.mult, [gl["k"].b[0], gl["e2"].b[0]], [GK.b[0]])
            O.tt("pool", GKE[:], gl["k"][:], gl["e3"][:], ALU.mult, [gl["k"].b[0], gl["e3"].b[0]], [GKE.b[0]])

        def gla_chunk(d, c, ci):
            cs_ = slice(c * CH, (c + 1) * CH)
            scp = nps()
            O.mm(scp[0:64, 0:64], GK[:, cs_], GQ[:, cs_], True, True, [GK.b[0], GQ.b[0]], [scp.b[0]])
            O.tt("dve", SC[:], scp[0:64, 0:64], cmask(1 if d == 0 else 3), ALU.mult, [scp.b[0], CST.b[0]], [SC.b[0]])
            O.tr(PSB[0:64, 0:128], GKE[:, cs_], IDB[:], [GKE.b[0], IDB.b[0]], [PSB.b[0]])
            O.cp("act", KE[:], PSB[0:64, 0:128], [PSB.b[0]], [KE.b[0]])
            op_, kvp = nps(), nps()
            O.mm(op_[0:64, 0:256], SC[:], VG[:, c, :], True, False, [SC.b[0], VG.b[0]], [op_.b[0]])
            O.mm(op_[0:64, 0:256], GQ[:, cs_], GSB[:], False, True, [GQ.b[0], GSB.b[0]], [op_.b[0]])
            O.mm(kvp[:, 0:256], KE[:], VG[:, c, :], True, True, [KE.b[0], VG.b[0]], [kvp.b[0]])
            yk = ci % 2
            O.cp("act", YT[:, yk, 256:512], op_[0:64, 0:256], [op_.b[0]], [YT.b[yk]])
            O.stt(GS[:], GS[:], DEC[:, c:c + 1], kvp[:, 0:256], ALU.mult, ALU.add, [GS.b[0], DEC.b[0], kvp.b[0]], [GS.b[0]])
            O.cp("pool", GSB[:], GS[:], [GS.b[0]], [GSB.b[0]])

        def finish_chunk(d, c, tg, ci):
            yk = ci % 2
            if d == 0:
                O.dma("pool", scr_y[tg:tg + CH, :], YT[:, yk, :], [YT.b[yk]], [], YT.b[yk])
                return
            O.dma("sp", YF[:, 0, :], scr_y[tg:tg + CH, :], [], [YF.b[0]], YF.b[0])
            O.tt("dve", YT[:, yk, :], YT[:, yk, :], YF[:, 0, :], ALU.add, [YT.b[yk], YF.b[0]], [YT.b[yk]])
            yv = YT[:, yk, 0:256].rearrange("p (u v) -> p u v", v=64)
            S.op("dve", lambda e, yv=yv: e.tensor_reduce(out=stat[:, 0:4], in_=yv, axis=AX.X, op=ALU.add), [YT.b[yk]], [stat.b[0]])
            O.ts("dve", stat[:, 0:4], stat[:, 0:4], 1.0 / 64, None, ALU.mult, None, [stat.b[0]], [stat.b[0]])
            onv = ON[:, 0:256].rearrange("p (u v) -> p u v", v=64)
            O.tt("dve", onv, yv, stat[:, 0:4].unsqueeze(2).to_broadcast([64, 4, 64]), ALU.subtract, [YT.b[yk], stat.b[0]], [ON.b[0]])
            O.tt("pool", TMP[:], onv, onv, ALU.mult, [ON.b[0]], [TMP.b[0]])
            S.op("dve", lambda e: e.tensor_reduce(out=stat[:, 4:8], in_=TMP[:], axis=AX.X, op=ALU.add), [TMP.b[0]], [stat.b[0]])
            O.ts("dve", stat[:, 4:8], stat[:, 4:8], 1.0 / 64, 64e-5, ALU.mult, ALU.add, [stat.b[0]], [stat.b[0]])
            O.act(stat[:, 4:8], stat[:, 4:8], AF.Sqrt, [stat.b[0]], [stat.b[0]])
            S.op("dve", lambda e: e.reciprocal(out=stat[:, 8:12], in_=stat[:, 4:8]), [stat.b[0]], [stat.b[0]])
            O.tt("dve", onv, onv, stat[:, 8:12].unsqueeze(2).to_broadcast([64, 4, 64]), ALU.mult, [ON.b[0], stat.b[0]], [ON.b[0]])
            O.act(ON[:, 256:512], YT[:, yk, 256:512], AF.Square, [YT.b[yk]], [ON.b[0], stat.b[0]], accum=stat[:, 12:13])
            O.ts("dve", stat[:, 13:14], stat[:, 12:13], 1.0 / 256, EPS, ALU.mult, ALU.add, [stat.b[0]], [stat.b[0]])
            O.act(stat[:, 13:14], stat[:, 13:14], AF.Sqrt, [stat.b[0]], [stat.b[0]])
            S.op("dve", lambda e: e.reciprocal(out=stat[:, 14:15], in_=stat[:, 13:14]), [stat.b[0]], [stat.b[0]])
            O.ts("dve", ON[:, 256:512], YT[:, yk, 256:512], stat[:, 14:15], None, ALU.mult, None, [YT.b[yk], stat.b[0]], [ON.b[0]])
            tp = nps()
            for u in range(4):
                O.tr(tp[0:64, u * 64:(u + 1) * 64], ON[:, u * 64:(u + 1) * 64], IDF[0:64, 0:64], [ON.b[0], CST.b[0]], [tp.b[0]])
            tg2 = nps()
            for c2 in range(2):
                O.tr(tg2[:, c2 * 64:(c2 + 1) * 64], ON[:, 256 + c2 * 128:256 + (c2 + 1) * 128], IDF[0:64, 0:64], [ON.b[0], CST.b[0]], [tg2.b[0]])
            cs_ = slice(c * CH, (c + 1) * CH)
            for u in range(4):
                O.ts("dve", TMP[:, u, :], tp[0:64, u * 64:(u + 1) * 64], prc(52 + u), prc(56 + u), ALU.mult, ALU.add, [tp.b[0], PR.b[0]], [TMP.b[0]])
            O.tt("pool", TMP[:], TMP[:], fm["bon"][:, :, cs_], ALU.add, [TMP.b[0], fm["bon"].b[0]], [TMP.b[0]])
            O.tt("pool", YO[0:64, :, :], TMP[:], fm["g"][:, :, cs_], ALU.mult, [TMP.b[0], fm["g"].b[0]], [YO.b[0]])
            O.dma("pool", y_d[0:256, tg:tg + CH].rearrange("(u p) t -> p u t", p=64), YO[0:64, :, :], [YO.b[0]], [], YO.b[0])
            YG = S.tile_cache("YG", [128, 2, CH], BF16)
            for c2 in range(2):
                O.stt(YG[:, c2, :], tg2[:, c2 * 64:(c2 + 1) * 64], PG[:, 14 + c2:15 + c2], SG[:, c2, cs_], ALU.mult, ALU.mult,
                      [tg2.b[0], PG.b[0], SG.b[0]], [YG.b[0]])
            O.dma("pool", y_d[256:512, tg:tg + CH].rearrange("(c p) t -> p c t", p=128), YG[:], [YG.b[0]], [], YG.b[0])

        if dbg:
            try:
                run_pass(0)
            except _Stop:
                pass
            S.wait_all("sp", YT.b + fm["bon"].b + ST.b)
            S.wait_all("act", YT.b + ST.b + STB.b)
            S.wait_all("dve", YT.b + ST.b + STB.b)
            S.wait_all("pool", YT.b + ST.b + STB.b)
        else:
            run_pass(0)
            S.wait_all("sp", YT.b + fm["bon"].b)
            run_pass(1)
            S.wait_all("sp", YO.b + S.tile_cache("YG", [128, 2, CH], BF16).b)
        S.emit()
    return nc


def _run(nc, in_maps):
    res = run_bass_kernel_spmd(nc, in_maps, core_ids=list(range(NCORES)))
    return res.results


def _mix0_consts():
    cst = np.zeros((128, 1280), np.float32)
    cst[:, 0:128] = np.eye(128, dtype=np.float32)
    r = np.arange(64)[:, None]
    c = np.arange(64)[None, :]
    for k, m in enumerate((r < c, r <= c, r > c, r >= c)):
        cst[0:64, 128 + k * 64:128 + (k + 1) * 64] = m
    t = np.arange(256)
    cst[:, 384:640] = (t % 64 != 0)[None, :]
    cst[:, 640:896] = (t % 64 != 63)[None, :]
    return cst


def _mix0_inputs(hT, g, Wb, p):
    A = 3712
    cols = np.concatenate([np.arange(256 * g, 256 * g + 256), 1024 + np.arange(256 * g, 256 * g + 256),
                           2048 + np.arange(256 * g, 256 * g + 256), np.arange(3072, 3712),
                           A + np.arange(128 * g, 128 * g + 128), A + 512 + np.arange(128 * g, 128 * g + 128),
                           A + 1024 + np.arange(256 * g, 256 * g + 256), A + 2048 + np.arange(256 * g, 256 * g + 256),
                           A + 3072 + np.arange(32)])
    win = np.ascontiguousarray(Wb["ab_w_in"][0][:, cols])
    g2 = np.ascontiguousarray(Wb["rw_g2"][0][:, 256 * g:256 * g + 256].reshape(2, 128, 256).transpose(1, 0, 2).reshape(128, 512))
    oc = slice(256 * g, 256 * g + 256)
    w2 = np.ascontiguousarray(np.stack([Wb["rw_w2"][0][0][:, oc], Wb["rw_w2"][0][1][:, oc], Wb["rw_a2"][0][0][:, oc],
                                        Wb["rw_a2"][0][1][:, oc]], 1).reshape(96, 1024))
    gw2 = np.ascontiguousarray(Wb["gla_gw2"][0][:, :, 128 * g:128 * g + 128].transpose(1, 0, 2).reshape(16, 256))
    mu = p["rw_mu"][0]
    pr = np.zeros((64, 64), np.float32)
    for q in range(3):
        for j in range(4):
            for m in range(2):
                pr[:, (q * 4 + j) * 2 + m] = mu[m, q * 1024 + (4 * g + j) * 64:q * 1024 + (4 * g + j + 1) * 64]
    for j in range(4):
        hs = slice((4 * g + j) * 64, (4 * g + j + 1) * 64)
        pr[:, 24 + j] = p["rw_kk"][0][hs]
        pr[:, 28 + j] = p["rw_ka"][0][hs]
        pr[:, 32 + j] = p["rw_rk"][0][4 * g + j]
        for dd in range(2):
            pr[:, 36 + dd * 4 + j] = p["rw_w0"][0][dd, hs]
            pr[:, 44 + dd * 4 + j] = p["rw_a0"][0][dd, hs]
        pr[:, 52 + j] = p["rw_ln_w"][0][hs]
        pr[:, 56 + j] = p["rw_ln_b"][0][hs]
    pg = np.zeros((128, 16), np.float32)
    for kc in range(2):
        for m in range(2):
            pg[:, kc * 2 + m] = mu[m, 3072 + kc * 128:3072 + (kc + 1) * 128]
    for grp, base in enumerate((3328, 3328 + 96, 3520, 3520 + 96)):
        for m in range(2):
            pg[0:96, 4 + grp * 2 + m] = mu[m, base:base + 96]
    for dd in range(2):
        pg[:, 12 + dd] = p["gla_gb"][0][dd, 128 * g:128 * g + 128]
    for c2 in range(2):
        pg[:, 14 + c2] = p["gla_norm"][0][c2 * 128:(c2 + 1) * 128]
    return {"h": hT, "win": win, "g2": g2, "w2": w2, "gw2": gw2, "pr": pr, "pg": pg, "cst": _mix0_consts(),
            "identb": np.eye(128, dtype=np.float32).astype(NPBF)}


def _mix1_inputs(hT, g, Wb, p):
    w_in = Wb["lru_w_in"][0]
    win = np.ascontiguousarray(np.concatenate([w_in[:, g * 512:(g + 1) * 512], w_in[:, 2048 + g * 512:2048 + (g + 1) * 512]], 1))
    wax = np.zeros((16, 128, 256), NPBF)
    for dd in range(2):
        for wi, Wm in enumerate((Wb["lru_wa"][0], Wb["lru_wx"][0])):
            for blk in range(2):
                for kc in range(2):
                    wax[((dd * 2 + wi) * 2 + blk) * 2 + kc] = Wm[dd, g * 2 + blk][kc * 128:(kc + 1) * 128, :]
    wax = np.ascontiguousarray(wax.transpose(1, 0, 2).reshape(128, 16 * 256))
    cs = slice(g * 512, (g + 1) * 512)
    qs = [p["lru_conv_w"][0][j, cs] for j in range(4)] + [p["lru_conv_b"][0][cs], p["lru_ba"][0][0, cs], p["lru_ba"][0][1, cs],
                                                          p["lru_bx"][0][0, cs], p["lru_bx"][0][1, cs], p["lru_lam"][0][0, cs],
                                                          p["lru_lam"][0][1, cs]]
    pv = np.ascontiguousarray(np.stack([q.reshape(4, 128).T for q in qs], 1).reshape(128, 44).astype(np.float32))
    return {"h": hT, "win": win, "wax": wax, "pv": pv}


_WNAMES = ["mlp_w1", "mlp_w2", "ab_w_in", "ab_w_out", "lru_w_in", "lru_w_out", "rw_w2", "rw_a2", "rw_g2", "gla_gw2", "lru_wa", "lru_wx"]


def kernel(**p):
    p = {k: np.asarray(v) for k, v in p.items()}
    B, SEQ = p["x"].shape[0], p["x"].shape[1]
    flat = np.concatenate([p[k].reshape(-1) for k in _WNAMES])
    tot = flat.size
    per = -(-tot // (NCORES * 128))
    FW = per
    padded = np.zeros(NCORES * 128 * FW, np.float32)
    padded[:tot] = flat
    shards = padded.reshape(NCORES, 128, FW)
    cvec = np.stack([p["c"][0], p["c"][1], p["c_ctx"]], 1)
    cT = np.ascontiguousarray(cvec.reshape(16, 128, 3).transpose(1, 0, 2).reshape(128, 48))
    ins = []
    for i in range(NCORES):
        layer, q = i // 4, i % 4
        ins.append({"wf": shards[i], "cT": cT,
                    "modw": np.ascontiguousarray(p["mod_w"][layer][:, q * MODC:(q + 1) * MODC]),
                    "modb": np.ascontiguousarray(p["mod_b"][layer][q * MODC:(q + 1) * MODC].reshape(1, MODC))})
    resA = _run(build_prep(FW), ins)
    wbf = np.concatenate([r["wb"].reshape(-1) for r in resA])[:tot]
    Wb = {}
    off = 0
    for k in _WNAMES:
        n = p[k].size
        Wb[k] = wbf[off:off + n].reshape(p[k].shape)
        off += n
    M = [np.concatenate([resA[layer * 4 + q]["m"] for q in range(4)], 1).reshape(3, 6, D) for layer in range(2)]

    def mrow(layer, cls, b):
        return M[layer][b if cls == "lat" else 2]
    ident = np.eye(128, dtype=np.float32).astype(NPBF)
    zeros = np.zeros(D, np.float32)

    def w1t_of(W1):
        return np.ascontiguousarray(W1.reshape(16, 128, 64, 128).transpose(2, 1, 0, 3).reshape(64, 128, D))

    def vec(layer, cls, b, mode):
        m = mrow(layer, cls, b)
        v = np.zeros((8, D), np.float32)
        v[0], v[1], v[2], v[3], v[4] = m[2], m[3], m[4], m[5], p["norm2"][layer]
        if mode == "last":
            v[7] = p["final_norm"]
        elif mode == "mid":
            m2 = mrow(layer + 1, cls, b)
            v[5], v[6], v[7] = m2[0], m2[1], p["norm1"][layer + 1]
        else:
            v[5], v[6], v[7] = m[0], m[1], p["norm1"][layer]
        return v
    NL, NC_ = SEQ // 4, CTX_LEN // 4
    ins = []
    for i in range(NCORES):
        b, q = i // 4, i % 4
        ins.append({"ident": ident, "x_lat": np.ascontiguousarray(p["x"][b, q * NL:(q + 1) * NL]), "vec_lat": vec(0, "lat", b, "pre"),
                    "x_ctx": np.ascontiguousarray(p["ctx"][b, q * NC_:(q + 1) * NC_]), "vec_ctx": vec(0, "ctx", b, "pre")})
    resB = _run(build_tok("pre", NL, NC_), ins)
    H0 = [np.ascontiguousarray(np.concatenate([resB[b * 4 + q]["h_ctx"] for q in range(4)] + [resB[b * 4 + q]["h_lat"] for q in range(4)], 1))
          for b in range(B)]
    ins = [_mix0_inputs(H0[i // 4], i % 4, Wb, p) for i in range(NCORES)]
    resC = _run(build_mix0(), ins)
    Y0 = []
    for b in range(B):
        y = np.zeros((D, CTX_LEN + SEQ), NPBF)
        for g in range(4):
            r = resC[b * 4 + g]["y"]
            y[256 * g:256 * g + 256] = r[0:256]
            y[1024 + 256 * g:1024 + 256 * g + 256] = r[256:512]
        Y0.append(y)
    w1t0, w1t1 = w1t_of(Wb["mlp_w1"][0]), w1t_of(Wb["mlp_w1"][1])
    ins = []
    for i in range(NCORES):
        b, q = i // 4, i % 4
        ins.append({"ident": ident, "x_lat": np.ascontiguousarray(p["x"][b, q * NL:(q + 1) * NL]), "vec_lat": vec(0, "lat", b, "mid"),
                    "x_ctx": np.ascontiguousarray(p["ctx"][b, q * NC_:(q + 1) * NC_]), "vec_ctx": vec(0, "ctx", b, "mid"),
                    "y_lat": np.ascontiguousarray(Y0[b][:, CTX_LEN + q * NL:CTX_LEN + (q + 1) * NL]),
                    "y_ctx": np.ascontiguousarray(Y0[b][:, q * NC_:(q + 1) * NC_]),
                    "wout": np.ascontiguousarray(Wb["ab_w_out"][0]), "w1t": w1t0, "w2": np.ascontiguousarray(Wb["mlp_w2"][0])})
    resD = _run(build_tok("mid", NL, NC_), ins)
    X1 = [np.concatenate([resD[b * 4 + q]["xo_lat"] for q in range(4)], 0) for b in range(B)]
    H1 = []
    rows = SEQ // GRID_W
    for b in range(B):
        hl = np.concatenate([resD[b * 4 + q]["h_lat"] for q in range(4)], 1)
        hl = hl.reshape(D, rows, GRID_W).transpose(0, 2, 1).reshape(D, SEQ)
        hc = np.concatenate([resD[b * 4 + q]["h_ctx"] for q in range(4)], 1)
        H1.append(np.ascontiguousarray(np.concatenate([hc, hl], 1)))
    ins = [_mix1_inputs(H1[i // 4], i % 4, Wb, p) for i in range(NCORES)]
    resE = _run(build_mix1(), ins)
    Y1 = []
    for b in range(B):
        y = np.concatenate([resE[b * 4 + g]["y"][:, CTX_LEN:] for g in range(4)], 0)
        y = y.reshape(D, GRID_W, rows).transpose(0, 2, 1).reshape(D, SEQ)
        Y1.append(y)
    ins = []
    for i in range(NCORES):
        b, q = i // 4, i % 4
        ins.append({"ident": ident, "x_lat": np.ascontiguousarray(X1[b][q * NL:(q + 1) * NL]), "vec_lat": vec(1, "lat", b, "last"),
                    "y_lat": np.ascontiguousarray(Y1[b][:, q * NL:(q + 1) * NL]),
                    "wout": np.ascontiguousarray(Wb["lru_w_out"][0]), "w1t": w1t1, "w2": np.ascontiguousarray(Wb["mlp_w2"][1])})
    resF = _run(build_tok("last", NL, 0), ins)
    out = np.stack([np.concatenate([resF[b * 4 + q]["out_lat"] for q in range(4)], 0) for b in range(B)], 0)
    return out.astype(np.float32)


CTX_LEN = 256
GRID_W = 64
```
